# Optimizing a Trainium2 kernel written in Bass

```python
import math
import jax, jax.numpy as jnp
from jax import lax
import numpy as np

D_MODEL = 1024
BATCH = 8
SEQ = 4096
DEPTH = 2

HEAD_DIM = 64
A_WIDTH = D_MODEL // 2
A_HEADS = A_WIDTH // HEAD_DIM
B_WIDTH = D_MODEL - A_WIDTH
DECAY_RANK = 64
ICL_RANK = 64
GATE_RANK = 128
CONV_WIDTH = 31
GN_EPS = 64e-5
A_PROJ = 3 * A_WIDTH + DECAY_RANK + ICL_RANK + GATE_RANK
EVEN_IN = A_PROJ + 2 * B_WIDTH
C_HEADS = D_MODEL // (2 * HEAD_DIM)
ODD_IN = 3 * D_MODEL
Q_BLOCK = 128
ROPE_THETA = 10000.0
ATTN_SCALE = HEAD_DIM ** -0.5
D_FF = 2816
N_EVEN = (DEPTH + 1) // 2
N_ODD = DEPTH // 2

kernel_name = "hybrid_rwkv7_conformer_diffattn_macaron"


def rmsnorm(x, g, eps=1e-6):
    xf = x.astype(jnp.float32)
    y = xf * lax.rsqrt(jnp.mean(xf * xf, axis=-1, keepdims=True) + eps)
    return (y * g.astype(jnp.float32)).astype(x.dtype)


def layernorm(x, g, b, eps=1e-5):
    xf = x.astype(jnp.float32)
    mu = jnp.mean(xf, axis=-1, keepdims=True)
    var = jnp.mean(jnp.square(xf - mu), axis=-1, keepdims=True)
    y = (xf - mu) * lax.rsqrt(var + eps)
    return (y * g.astype(jnp.float32) + b.astype(jnp.float32)).astype(x.dtype)


def swiglu(h, w_gate, w_up, w_down):
    return (jax.nn.silu(h @ w_gate) * (h @ w_up)) @ w_down


def token_shift(z):
    return jnp.pad(z, ((0, 0), (1, 0), (0, 0)))[:, :-1]


def rope_tables(seq):
    inv = 1.0 / (ROPE_THETA ** (jnp.arange(0, HEAD_DIM, 2, dtype=jnp.float32) / HEAD_DIM))
    ang = jnp.arange(seq, dtype=jnp.float32)[:, None] * inv[None, :]
    return jnp.cos(ang), jnp.sin(ang)


def apply_rope(x, cos, sin):
    xf = x.astype(jnp.float32)
    c = cos[None, :, None, None, :]
    s = sin[None, :, None, None, :]
    x1, x2 = xf[..., : HEAD_DIM // 2], xf[..., HEAD_DIM // 2:]
    return jnp.concatenate([x1 * c - x2 * s, x2 * c + x1 * s], axis=-1)


def wkv7_scan(r, w, k, v, a, b):
    Bsz, _, H, N = r.shape
    xs = tuple(jnp.moveaxis(t, 1, 0) for t in (r, w, k, v, a, b))

    def step(state, inp):
        r_t, w_t, k_t, v_t, a_t, b_t = inp
        sa = jnp.einsum('bhvk,bhk->bhv', state, a_t)
        state = (state * w_t[:, :, None, :]
                 + sa[..., None] * b_t[:, :, None, :]
                 + v_t[..., None] * k_t[:, :, None, :])
        y = jnp.einsum('bhvk,bhk->bhv', state, r_t)
        return state, y

    s0 = jnp.zeros((Bsz, H, N, N), jnp.float32)
    _, ys = lax.scan(step, s0, xs)
    return jnp.moveaxis(ys, 0, 1)


def rwkv_conv_mixer(h, w_in, mu, w0, w2, a0, a2, g2, k_k, k_a, r_k, ln_w, ln_b,
                    glu_bias, dw, dw_bias, cln_w, cln_b, w_out):
    Bsz, S, _ = h.shape
    f32 = jnp.float32
    z = h @ w_in
    za = z[..., :A_PROJ]
    za = za + (token_shift(za) - za) * mu
    cuts = [A_WIDTH, 2 * A_WIDTH, 3 * A_WIDTH, 3 * A_WIDTH + DECAY_RANK,
            3 * A_WIDTH + DECAY_RANK + ICL_RANK]
    r, k, v, wd, ad, gd = jnp.split(za, cuts, axis=-1)
    w_log = -jax.nn.softplus(-(w0 + jnp.tanh(wd) @ w2)) - 0.5
    decay = jnp.exp(-jnp.exp(w_log.astype(f32)))
    a = jax.nn.sigmoid(a0 + ad @ a2)
    g = jax.nn.sigmoid(gd) @ g2

    def heads(t):
        return t.reshape(Bsz, S, A_HEADS, HEAD_DIM).astype(f32)

    kk = heads(k * k_k)
    kk = kk / jnp.maximum(jnp.sqrt(jnp.sum(kk * kk, axis=-1, keepdims=True)), 1e-12)
    k = k * (1.0 + (a - 1.0) * k_a)
    rh, kh, vh, ah, wh = heads(r), heads(k), heads(v), heads(a), heads(decay)
    y = wkv7_scan(rh, wh, kh, vh, -kk, kk * ah)
    mean = jnp.mean(y, axis=-1, keepdims=True)
    var = jnp.mean(jnp.square(y - mean), axis=-1, keepdims=True)
    y = ((y - mean) * lax.rsqrt(var + GN_EPS)).reshape(Bsz, S, A_WIDTH)
    y = y * ln_w.astype(f32) + ln_b.astype(f32)
    bonus = (jnp.sum(rh * kh * r_k.astype(f32), axis=-1, keepdims=True) * vh).reshape(Bsz, S, A_WIDTH)
    y_a = ((y + bonus) * g.astype(f32)).astype(h.dtype)
    u = z[..., A_PROJ:] + glu_bias
    gl = u[..., :B_WIDTH] * jax.nn.sigmoid(u[..., B_WIDTH:])
    c = lax.conv_general_dilated(gl, dw[:, None, :], window_strides=(1,),
                                 padding=[(CONV_WIDTH - 1, 0)],
                                 dimension_numbers=('NWC', 'WIO', 'NWC'),
                                 feature_group_count=B_WIDTH) + dw_bias
    y_b = jax.nn.silu(layernorm(c, cln_w, cln_b))
    return jnp.concatenate([y_a, y_b], axis=-1) @ w_out


def diff_attention(h, w_in, q_norm, k_norm, lq1, lk1, lq2, lk2, sub_norm, w_out,
                   lam_init, cos, sin):
    Bsz, S, _ = h.shape
    f32 = jnp.float32
    z = h @ w_in
    q = z[..., :D_MODEL].reshape(Bsz, S, C_HEADS, 2, HEAD_DIM)
    k = z[..., D_MODEL:2 * D_MODEL].reshape(Bsz, S, C_HEADS, 2, HEAD_DIM)
    v = z[..., 2 * D_MODEL:].reshape(Bsz, S, C_HEADS, 2 * HEAD_DIM)
    q = apply_rope(rmsnorm(q, q_norm), cos, sin).transpose(3, 0, 2, 1, 4)
    k = apply_rope(rmsnorm(k, k_norm), cos, sin).transpose(3, 0, 2, 1, 4)
    vf = v.transpose(0, 2, 1, 3).astype(f32)
    lam = (jnp.exp(jnp.sum(lq1.astype(f32) * lk1.astype(f32)))
           - jnp.exp(jnp.sum(lq2.astype(f32) * lk2.astype(f32))) + lam_init)
    nb = S // Q_BLOCK
    qb = q.reshape(2, Bsz, C_HEADS, nb, Q_BLOCK, HEAD_DIM).transpose(3, 0, 1, 2, 4, 5)
    kpos = jnp.arange(S)

    def block(args):
        qi, i = args
        s = jnp.einsum('cbhqd,cbhkd->cbhqk', qi, k) * ATTN_SCALE
        qpos = i * Q_BLOCK + jnp.arange(Q_BLOCK)
        mask = kpos[None, :] <= qpos[:, None]
        p = jax.nn.softmax(jnp.where(mask, s, -jnp.inf), axis=-1)
        attn = p[0] - lam * p[1]
        return jnp.einsum('bhqk,bhkv->bhqv', attn, vf)

    o = lax.map(block, (qb, jnp.arange(nb)))
    o = o.transpose(1, 0, 3, 2, 4).reshape(Bsz, S, C_HEADS, 2 * HEAD_DIM)
    o = rmsnorm(o, sub_norm, eps=1e-5) * (1.0 - lam_init)
    return o.reshape(Bsz, S, D_MODEL).astype(h.dtype) @ w_out


def setup_inputs(seed: int = 0) -> dict:
    key = jax.random.key(seed)
    ks = iter(jax.random.split(key, 40))
    f32 = jnp.float32

    def nrm(shape, scale):
        return jax.random.normal(next(ks), shape, f32) * scale

    def gain(shape):
        return 1.0 + nrm(shape, 0.02)

    return {
        "x": nrm((BATCH, SEQ, D_MODEL), 1.0),
        "ffn_norm": gain((DEPTH, 2, D_MODEL)),
        "ffn_w_gate": nrm((DEPTH, 2, D_MODEL, D_FF), D_MODEL ** -0.5),
        "ffn_w_up": nrm((DEPTH, 2, D_MODEL, D_FF), D_MODEL ** -0.5),
        "ffn_w_down": nrm((DEPTH, 2, D_FF, D_MODEL), D_FF ** -0.5),
        "mix_norm": gain((DEPTH, D_MODEL)),
        "a_w_in": nrm((N_EVEN, D_MODEL, EVEN_IN), D_MODEL ** -0.5),
        "a_mu": jax.random.uniform(next(ks), (N_EVEN, A_PROJ), f32, 0.0, 1.0),
        "a_w0": jax.random.uniform(next(ks), (N_EVEN, A_WIDTH), f32, -3.0, 1.0),
        "a_w2": nrm((N_EVEN, DECAY_RANK, A_WIDTH), DECAY_RANK ** -0.5),
        "a_a0": nrm((N_EVEN, A_WIDTH), 0.1),
        "a_a2": nrm((N_EVEN, ICL_RANK, A_WIDTH), ICL_RANK ** -0.5),
        "a_g2": nrm((N_EVEN, GATE_RANK, A_WIDTH), GATE_RANK ** -0.5),
        "a_k_k": 0.85 + nrm((N_EVEN, A_WIDTH), 0.05),
        "a_k_a": 1.0 + nrm((N_EVEN, A_WIDTH), 0.05),
        "a_r_k": nrm((N_EVEN, A_HEADS, HEAD_DIM), 0.1),
        "a_ln_w": gain((N_EVEN, A_WIDTH)),
        "a_ln_b": nrm((N_EVEN, A_WIDTH), 0.02),
        "b_glu_bias": nrm((N_EVEN, 2 * B_WIDTH), 0.02),
        "b_dw": nrm((N_EVEN, CONV_WIDTH, B_WIDTH), CONV_WIDTH ** -0.5),
        "b_dw_bias": nrm((N_EVEN, B_WIDTH), 0.02),
        "b_ln_w": gain((N_EVEN, B_WIDTH)),
        "b_ln_b": nrm((N_EVEN, B_WIDTH), 0.02),
        "e_w_out": nrm((N_EVEN, D_MODEL, D_MODEL), D_MODEL ** -0.5),
        "c_w_in": nrm((N_ODD, D_MODEL, ODD_IN), D_MODEL ** -0.5),
        "c_q_norm": gain((N_ODD, HEAD_DIM)),
        "c_k_norm": gain((N_ODD, HEAD_DIM)),
        "c_lq1": nrm((N_ODD, HEAD_DIM), 0.1),
        "c_lk1": nrm((N_ODD, HEAD_DIM), 0.1),
        "c_lq2": nrm((N_ODD, HEAD_DIM), 0.1),
        "c_lk2": nrm((N_ODD, HEAD_DIM), 0.1),
        "c_sub_norm": gain((N_ODD, 2 * HEAD_DIM)),
        "c_w_out": nrm((N_ODD, D_MODEL, D_MODEL), D_MODEL ** -0.5),
    }


def reference(x, ffn_norm, ffn_w_gate, ffn_w_up, ffn_w_down, mix_norm,
              a_w_in, a_mu, a_w0, a_w2, a_a0, a_a2, a_g2, a_k_k, a_k_a, a_r_k,
              a_ln_w, a_ln_b, b_glu_bias, b_dw, b_dw_bias, b_ln_w, b_ln_b, e_w_out,
              c_w_in, c_q_norm, c_k_norm, c_lq1, c_lk1, c_lq2, c_lk2, c_sub_norm,
              c_w_out):
    cos, sin = rope_tables(x.shape[1])
    for layer in range(DEPTH):
        x = x + 0.5 * swiglu(rmsnorm(x, ffn_norm[layer, 0]), ffn_w_gate[layer, 0],
                             ffn_w_up[layer, 0], ffn_w_down[layer, 0])
        h = rmsnorm(x, mix_norm[layer])
        j = layer // 2
        if layer % 2 == 0:
            x = x + rwkv_conv_mixer(h, a_w_in[j], a_mu[j], a_w0[j], a_w2[j], a_a0[j],
                                    a_a2[j], a_g2[j], a_k_k[j], a_k_a[j], a_r_k[j],
                                    a_ln_w[j], a_ln_b[j], b_glu_bias[j], b_dw[j],
                                    b_dw_bias[j], b_ln_w[j], b_ln_b[j], e_w_out[j])
        else:
            lam_init = 0.8 - 0.6 * math.exp(-0.3 * layer)
            x = x + diff_attention(h, c_w_in[j], c_q_norm[j], c_k_norm[j], c_lq1[j],
                                   c_lk1[j], c_lq2[j], c_lk2[j], c_sub_norm[j],
                                   c_w_out[j], lam_init, cos, sin)
        x = x + 0.5 * swiglu(rmsnorm(x, ffn_norm[layer, 1]), ffn_w_gate[layer, 1],
                             ffn_w_up[layer, 1], ffn_w_down[layer, 1])
    return x
```

```python
import math
from contextlib import ExitStack
import numpy as np
import concourse.bass as bass
import concourse.mybir as mybir
from concourse.bass_utils import run_bass_kernel_spmd

F32 = mybir.dt.float32
BF16 = mybir.dt.bfloat16
AF = mybir.ActivationFunctionType
ALU = mybir.AluOpType
AX = mybir.AxisListType

S = 4096
D = 1024
DFF = 2816
NFC = DFF // 128
SAME_ENG_SYNC = True
SEM_LIMIT = 30000


class Trk:
    __slots__ = ("lw", "rd", "dsem", "dcnt", "name", "psum")

    def __init__(self, name=""):
        self.psum = False
        self.lw = None
        self.rd = {}
        self.dsem = None
        self.dcnt = 0
        self.name = name


class V:
    __slots__ = ("ap", "trk")

    def __init__(self, ap, trk):
        self.ap = ap
        self.trk = trk


class Buf:
    def __init__(self, t, name=""):
        self.t = t
        self.trk = Trk(name)

    def __getitem__(self, idx):
        return V(self.t[idx], self.trk)

    def v(self, ap):
        return V(ap, self.trk)


class KB:
    def __init__(self, nc, es):
        self.nc = nc
        self.es = es
        self.E = {"pe": nc.tensor, "act": nc.scalar, "dve": nc.vector, "pool": nc.gpsimd, "sp": nc.sync}
        self.sem = {}
        self.cnt = {}
        self.seen = {e: {} for e in self.E}
        self.nsem = 0
        for e in ("pe", "act", "dve", "pool"):
            self.sem[e] = self.newsem(e)
            self.cnt[e] = 0
        self.ninst = 0
        self.scope = es
        self.dma_trks = []
        self.scope_trks = []
        self.free_dsems = []

    def push_scope(self):
        self.scope = ExitStack()
        self.scope_trks = []
        return self.scope

    def pop_scope(self):
        self.barrier()
        for t in self.scope_trks:
            if t.dsem is not None:
                self.free_dsems.append((t.dsem, t.dcnt))
                if t in self.dma_trks:
                    self.dma_trks.remove(t)
                t.dsem = None
        self.scope_trks = []
        self.scope.close()
        self.scope = self.es

    def barrier(self):
        toks = [(self.sem[e], self.cnt[e], e) for e in ("pe", "act", "dve", "pool") if self.cnt[e] > 0]
        toks += [(t.dsem, t.dcnt, "dma") for t in self.dma_trks]
        for e in self.E:
            self._wait(e, toks)

    def newsem(self, name):
        self.nsem += 1
        return self.es.enter_context(self.nc.semaphore(f"s{self.nsem}_{name}"))

    def uniq(self, name):
        self.nname = getattr(self, "nname", 0) + 1
        return f"{name}_{self.nname}"

    def sb(self, name, shape, dt):
        b = Buf(self.scope.enter_context(self.nc.sbuf_tensor(self.uniq(name), list(shape), dt)), name)
        if self.scope is not self.es:
            self.scope_trks.append(b.trk)
        return b

    def ps(self, name, shape=(128, 512), dt=F32):
        b = Buf(self.scope.enter_context(self.nc.psum_tensor(self.uniq(name), list(shape), dt)), name)
        b.trk.psum = True
        return b

    def dram(self, name, shape, dt, kind="Internal"):
        return Buf(self.nc.dram_tensor(name, list(shape), dt, kind=kind).ap(), name)

    def _deps(self, outs, ins, eng=None):
        deps = []
        for v in ins:
            if v.trk.lw is not None:
                deps.append(v.trk.lw)
            if v.trk.psum:
                deps.extend(t for e, t in v.trk.rd.items() if e != eng)
        for v in outs:
            if v.trk.lw is not None:
                deps.append(v.trk.lw)
            deps.extend(v.trk.rd.values())
        return deps

    def _wait(self, eng, deps):
        seen = self.seen[eng]
        best = {}
        for (sem, val, src) in deps:
            if src == eng and (eng == "pe" or not SAME_ENG_SYNC):
                continue
            key = id(sem)
            if seen.get(key, 0) >= val:
                continue
            if key not in best or best[key][1] < val:
                best[key] = (sem, val)
        for key, (sem, val) in best.items():
            self.E[eng].wait_ge(sem, val)
            seen[key] = val

    def op(self, eng, fn, outs, ins):
        outs = [o for o in outs if o is not None]
        ins = [i for i in ins if isinstance(i, V)]
        self._wait(eng, self._deps(outs, ins, eng))
        if self.cnt[eng] >= SEM_LIMIT:
            self.sem[eng] = self.newsem(eng)
            self.cnt[eng] = 0
        inst = fn()
        self.cnt[eng] += 1
        self.ninst += 1
        inst.then_inc(self.sem[eng], 1)
        tok = (self.sem[eng], self.cnt[eng], eng)
        for v in ins:
            v.trk.rd[eng] = tok
        for v in outs:
            v.trk.lw = tok
            v.trk.rd = {}
        return inst

    def dma(self, q, out, in_, **kw):
        self._wait(q, self._deps([out], [in_]))
        trk = out.trk
        if trk.dsem is None:
            if self.free_dsems:
                trk.dsem, trk.dcnt = self.free_dsems.pop()
            else:
                trk.dsem = self.newsem("d" + trk.name)
            self.dma_trks.append(trk)
        inst = self.E[q].dma_start(out=out.ap, in_=in_.ap, **kw)
        trk.dcnt += 16
        inst.then_inc(trk.dsem, 16)
        self.ninst += 1
        tok = (trk.dsem, trk.dcnt, "dma")
        in_.trk.rd["dma" + str(id(trk))] = tok
        out.trk.lw = tok
        out.trk.rd = {}
        return inst

    def mm(self, out, lhsT, rhs, start=True, stop=True, **kw):
        return self.op("pe", lambda: self.nc.tensor.matmul(out.ap, lhsT.ap, rhs.ap, start=start, stop=stop, **kw),
                       [out], [lhsT, rhs])

    def tr(self, out, in_, ident):
        return self.op("pe", lambda: self.nc.tensor.transpose(out.ap, in_.ap, ident.ap), [out], [in_, ident])

    def act(self, out, in_, func, bias=None, scale=None, accum_out=None, eng="act"):
        kw = {}
        if bias is not None:
            kw["bias"] = bias.ap if isinstance(bias, V) else bias
        if scale is not None:
            kw["scale"] = scale.ap if isinstance(scale, V) else scale
        if accum_out is not None:
            kw["accum_out"] = accum_out.ap
        return self.op("act", lambda: self.nc.scalar.activation(out.ap, in_.ap, func, **kw),
                       [out, accum_out], [in_, bias, scale])

    def ts(self, eng, out, in0, s1, s2, op0, op1=None, accum_out=None):
        e = self.E[eng]
        a1 = s1.ap if isinstance(s1, V) else s1
        a2 = s2.ap if isinstance(s2, V) else s2
        kw = {}
        if op1 is not None:
            kw["op1"] = op1
        if accum_out is not None:
            kw["accum_out"] = accum_out.ap
        return self.op(eng, lambda: e.tensor_scalar(out.ap, in0.ap, a1, a2, op0, **kw),
                       [out, accum_out], [in0, s1, s2])

    def tt(self, eng, out, in0, in1, op):
        e = self.E[eng]
        return self.op(eng, lambda: e.tensor_tensor(out.ap, in0.ap, in1.ap, op), [out], [in0, in1])

    def stt(self, out, in0, scalar, in1, op0, op1, eng="dve"):
        e = self.E[eng]
        sc = scalar.ap if isinstance(scalar, V) else scalar
        return self.op(eng, lambda: e.scalar_tensor_tensor(out.ap, in0.ap, sc, in1.ap, op0, op1),
                       [out], [in0, scalar, in1])

    def copy(self, eng, out, in_):
        e = self.E[eng]
        if eng == "act":
            return self.op(eng, lambda: e.copy(out.ap, in_.ap), [out], [in_])
        return self.op(eng, lambda: e.tensor_copy(out.ap, in_.ap), [out], [in_])

    def memset(self, eng, out, val):
        e = self.E[eng]
        return self.op(eng, lambda: e.memset(out.ap, val), [out], [])

    def recip(self, out, in_):
        return self.op("dve", lambda: self.nc.vector.reciprocal(out.ap, in_.ap), [out], [in_])

    def finish(self, trks):
        for t in trks:
            if t.dsem is not None:
                self.E["sp"].wait_ge(t.dsem, t.dcnt)


class Ctx:
    pass


def setup_common(k, c):
    nc = k.nc
    c.ident = k.sb("ident", [128, 128], F32)
    k.memset("pool", c.ident[:], 0.0)
    k.op("pool", lambda: nc.gpsimd.affine_select(c.ident.t[:], c.ident.t[:], pattern=[[-1, 128]],
                                                  compare_op=ALU.not_equal, fill=1.0, base=0,
                                                  channel_multiplier=1),
         [c.ident[:]], [c.ident[:]])
    c.mhalf = k.sb("mhalf", [128, 32], F32)
    k.memset("pool", c.mhalf[:, :], -0.5)


def alloc_ffn(k, c):
    c.wg = k.sb("wg", [128, 8, DFF], BF16)
    c.wu = k.sb("wu", [128, 8, DFF], BF16)
    c.wd = k.sb("wd", [128, NFC, D], BF16)
    c.xs = [k.sb(f"xs{i}", [128, D], F32) for i in range(2)]
    c.xr = [k.sb(f"xr{i}", [128, D], F32) for i in range(2)]
    c.xn = [k.sb(f"xn{i}", [128, D], F32) for i in range(2)]
    c.hT = [k.sb(f"hT{i}", [128, 8, 512], BF16) for i in range(2)]
    c.gT = k.sb("gT", [128, NFC, 512], BF16)
    c.sg = [k.sb(f"sg{i}", [128, 512], F32) for i in range(2)]
    c.ss = [k.sb(f"ss{i}", [128, 1], F32) for i in range(2)]
    c.gcol = k.sb("gcol", [128, 8], F32)
    c.ps = [k.ps(f"ps{i}") for i in range(8)]


def ffn_phase(k, c, xin, xout, wgate, wup, wdown, gnorm, first=False):
    nc = k.nc
    wgv = wgate.rearrange("(c p) f -> p c f", p=128)
    wuv = wup.rearrange("(c p) f -> p c f", p=128)
    wdv = wdown.rearrange("(c p) f -> p c f", p=128)
    H = DFF // 2
    for ci in range(8):
        for hh in range(2):
            k.dma("pool", c.wg[:, ci, hh * H:(hh + 1) * H], V(wgv[:, ci, hh * H:(hh + 1) * H], c.wtrk))
            k.dma("pool", c.wu[:, ci, hh * H:(hh + 1) * H], V(wuv[:, ci, hh * H:(hh + 1) * H], c.wtrk))
    for fi in range(NFC):
        k.dma("pool", c.wd[:, fi, :], V(wdv[:, fi, :], c.wtrk))
    k.dma("sp", c.gcol[:, :], V(gnorm.rearrange("(c p) -> p c", p=128), c.wtrk), allow_slow_non_contiguous=True)

    NT = S // 512
    cnt = [0]

    def norm_a(ti, s):
        i = cnt[0] % 2
        xs, xn, ss = c.xs[i], c.xn[i], c.ss[i]
        r0 = ti * 512 + s * 128
        k.dma("sp", xs[:, :], xin[r0:r0 + 128, :])
        rmsnorm_tile(k, c, xs[:, :], xn[:, :], ss[:, :], D, 1e-6)
        return i

    def norm_b(ti, s, i):
        xn = c.xn[i]
        hT = c.hT[ti % 2]
        cnt[0] += 1
        for half in range(2):
            pb = c.ps[half]
            for j in range(4):
                ci = half * 4 + j
                k.tr(pb[:, j * 128:(j + 1) * 128], xn[:, ci * 128:(ci + 1) * 128], c.ident[:, :])
            for j in range(4):
                ci = half * 4 + j
                if j % 2 == 0:
                    k.ts("dve", hT[:, ci, s * 128:(s + 1) * 128], pb[:, j * 128:(j + 1) * 128],
                         c.gcol[:, ci:ci + 1], None, ALU.mult)
                else:
                    k.act(hT[:, ci, s * 128:(s + 1) * 128], pb[:, j * 128:(j + 1) * 128],
                          AF.Copy, scale=c.gcol[:, ci:ci + 1])

    def gateup(ti):
        hT = c.hT[ti % 2]
        pend = {}
        for f in range(NFC):
            pg = c.ps[2 + (f % 2)]
            pu = c.ps[4 + (f % 2)]
            for ci in range(8):
                k.mm(pg[:, :], c.wg[:, ci, f * 128:(f + 1) * 128], hT[:, ci, :], start=(ci == 0), stop=(ci == 7))
            for ci in range(8):
                k.mm(pu[:, :], c.wu[:, ci, f * 128:(f + 1) * 128], hT[:, ci, :], start=(ci == 0), stop=(ci == 7))
            sg = c.sg[f % 2]
            k.act(sg[:, :], pg[:, :], AF.Silu)
            k.tt("dve", c.gT[:, f, :], sg[:, :], pu[:, :], ALU.mult)
            if ti + 1 < NT:
                if f in (1, 6, 11, 16):
                    s = (1, 6, 11, 16).index(f)
                    pend[s] = norm_a(ti + 1, s)
                if f in (5, 10, 15, 20):
                    s = (5, 10, 15, 20).index(f)
                    norm_b(ti + 1, s, pend[s])

    def down(ti):
        for s in range(4):
            r0 = ti * 512 + s * 128
            xr = c.xr[s % 2]
            k.dma("sp", xr[:, :], xin[r0:r0 + 128, :])
            for dh in range(2):
                po = c.ps[6 + dh]
                for f in range(NFC):
                    k.mm(po[:, :], c.gT[:, f, s * 128:(s + 1) * 128], c.wd[:, f, dh * 512:(dh + 1) * 512],
                         start=(f == 0), stop=(f == NFC - 1))
                k.stt(xr[:, dh * 512:(dh + 1) * 512], po[:, :], 0.5, xr[:, dh * 512:(dh + 1) * 512],
                      ALU.mult, ALU.add)
            k.dma("pool", xout[r0:r0 + 128, :], xr[:, :])

    for s in range(4):
        i = norm_a(0, s)
        norm_b(0, s, i)
    for ti in range(NT):
        gateup(ti)
        down(ti)


LAM_INIT1 = 0.8 - 0.6 * math.exp(-0.3 * 1)
ATT_SCALE = 64 ** -0.5


def rmsnorm_tile(k, c, xs, xn, ss, dim, eps):
    k.act(xn, xs, AF.Square, accum_out=ss)
    k.ts("dve", ss, ss, 1.0 / dim, eps, ALU.mult, ALU.add)
    k.tt("pool", ss, ss, c.mhalf[:, 0:1], ALU.pow)
    k.ts("dve", xn, xs, ss, None, ALU.mult)


def attn_layer(k, c, xin, xout, I):
    nc = k.nc
    sc = k.push_scope()
    W = c.wtrk
    NTT = S // 128
    qkT_d = k.dram("qkT_d", [16, 128, S], BF16)
    v_d = k.dram("v_d", [S, D], BF16)
    oT_d = k.dram("oT_d", [8, 128, S], BF16)
    ident = c.ident
    identb = k.sb("identb", [128, 128], BF16)
    k.copy("dve", identb[:, :], ident[:, :])
    win = k.sb("a_win", [128, 8, 3 * D], BF16)
    wout = k.sb("a_wout", [128, 8, D], BF16)
    winv = I["c_w_in"][0].rearrange("(c p) f -> p c f", p=128)
    woutv = I["c_w_out"][0].rearrange("(c p) f -> p c f", p=128)
    for ci in range(8):
        for hh in range(2):
            k.dma("pool", win[:, ci, hh * 1536:(hh + 1) * 1536], V(winv[:, ci, hh * 1536:(hh + 1) * 1536], W))
    for ci in range(8):
        k.dma("pool", wout[:, ci, :], V(woutv[:, ci, :], W))
    gcol = k.sb("a_gcol", [128, 8], F32)
    k.dma("sp", gcol[:, :], V(I["mix_norm"][1].rearrange("(c p) -> p c", p=128), W), allow_slow_non_contiguous=True)
    def bc_load(name, ap, n):
        t = k.sb(name, [128, n], F32)
        k.dma("sp", t[:, :], V(ap.partition_broadcast(128), W))
        return t
    qg = bc_load("a_qg", I["c_q_norm"][0], 64)
    kg = bc_load("a_kg", I["c_k_norm"][0], 64)
    lq1 = bc_load("a_lq1", I["c_lq1"][0], 64)
    lk1 = bc_load("a_lk1", I["c_lk1"][0], 64)
    lq2 = bc_load("a_lq2", I["c_lq2"][0], 64)
    lk2 = bc_load("a_lk2", I["c_lk2"][0], 64)
    gsub = bc_load("a_gsub", I["c_sub_norm"][0], 128)
    k.ts("dve", gsub[:, :], gsub[:, :], 1.0 - LAM_INIT1, None, ALU.mult)
    e1 = k.sb("a_e1", [128, 1], F32)
    e2 = k.sb("a_e2", [128, 1], F32)
    nlam = k.sb("a_nlam", [128, 1], F32)
    k.tt("dve", lq1[:, :], lq1[:, :], lk1[:, :], ALU.mult)
    k.op("dve", lambda: nc.vector.tensor_reduce(out=e1.t[:, :], in_=lq1.t[:, :], axis=AX.X, op=ALU.add), [e1[:, :]], [lq1[:, :]])
    k.tt("dve", lq2[:, :], lq2[:, :], lk2[:, :], ALU.mult)
    k.op("dve", lambda: nc.vector.tensor_reduce(out=e2.t[:, :], in_=lq2.t[:, :], axis=AX.X, op=ALU.add), [e2[:, :]], [lq2[:, :]])
    k.act(e1[:, :], e1[:, :], AF.Exp)
    k.act(e2[:, :], e2[:, :], AF.Exp)
    k.tt("dve", nlam[:, :], e2[:, :], e1[:, :], ALU.subtract)
    k.ts("dve", nlam[:, :], nlam[:, :], -LAM_INIT1, None, ALU.add)
    cosb = k.sb("a_cos", [128, NTT, 32], F32)
    sinb = k.sb("a_sin", [128, NTT, 32], F32)
    k.dma("sp", cosb[:, :, :], V(I["rope_cos"].rearrange("(t p) j -> p t j", p=128), W))
    k.dma("sp", sinb[:, :, :], V(I["rope_sin"].rearrange("(t p) j -> p t j", p=128), W))
    trif = k.sb("a_trif", [128, 128], F32)
    tri = k.sb("a_tri", [128, 128], BF16)
    k.memset("pool", trif[:, :], 1.0)
    k.op("pool", lambda: nc.gpsimd.affine_select(trif.t[:, :], trif.t[:, :], pattern=[[1, 128]],
                                                  compare_op=ALU.is_ge, fill=0.0, base=0, channel_multiplier=-1),
         [trif[:, :]], [trif[:, :]])
    k.copy("pool", tri[:, :], trif[:, :])

    ps_t = [k.ps(f"a_pst{i}") for i in range(2)]
    ps_z = [k.ps(f"a_psz{i}") for i in range(4)]
    ps_b = [k.ps(f"a_psb{i}", (128, 1024), BF16) for i in range(2)]
    xs2 = [k.sb(f"a_xs{i}", [128, D], F32) for i in range(2)]
    xn = k.sb("a_xn", [128, D], F32)
    ssn = k.sb("a_ssn", [128, 1], F32)
    hTt = [k.sb(f"a_hT{i}", [128, 8, 128], BF16) for i in range(2)]
    sq = k.sb("a_sq", [128, 512], F32)
    ssg = k.sb("a_ssg", [128, 32], F32)
    zn = [k.sb(f"a_zn{i}", [128, 8, 64], F32) for i in range(2)]
    tmpa = [k.sb(f"a_ta{i}", [128, 8, 32], F32) for i in range(2)]
    tmpb = [k.sb(f"a_tb{i}", [128, 8, 32], F32) for i in range(2)]
    cg = k.sb("a_cg", [128, 2, 2, 32], F32)
    sg_ = k.sb("a_sg", [128, 2, 2, 32], F32)
    qkr = [k.sb(f"a_qkr{i}", [128, 16, 128], BF16) for i in range(2)]
    qkT = [k.sb(f"a_qkT{i}", [128, 16, 128], BF16) for i in range(2)]
    vt = [k.sb(f"a_vt{i}", [128, D], BF16) for i in range(2)]
    qkT_dv = qkT_d.t.rearrange("j p s -> p j s")

    def a1_part1(t):
        xs = xs2[t % 2]
        hT = hTt[t % 2]
        k.dma("sp", xs[:, :], xin[t * 128:(t + 1) * 128, :])
        rmsnorm_tile(k, c, xs[:, :], xn[:, :], ssn[:, :], D, 1e-6)
        for half in range(2):
            pb = ps_t[half]
            for j in range(4):
                ci = half * 4 + j
                k.tr(pb[:, j * 128:(j + 1) * 128], xn[:, ci * 128:(ci + 1) * 128], ident[:, :])
            for j in range(4):
                ci = half * 4 + j
                if j % 2 == 0:
                    k.ts("dve", hT[:, ci, :], pb[:, j * 128:(j + 1) * 128], gcol[:, ci:ci + 1], None, ALU.mult)
                else:
                    k.act(hT[:, ci, :], pb[:, j * 128:(j + 1) * 128], AF.Copy, scale=gcol[:, ci:ci + 1])
        for w_, g_ in ((0, qg), (1, kg)):
            k.tt("pool", cg[:, w_, 0, :], cosb[:, t, :], g_[:, 0:32], ALU.mult)
            k.tt("pool", cg[:, w_, 1, :], cosb[:, t, :], g_[:, 32:64], ALU.mult)
            k.tt("pool", sg_[:, w_, 0, :], sinb[:, t, :], g_[:, 32:64], ALU.mult)
            k.tt("pool", sg_[:, w_, 1, :], sinb[:, t, :], g_[:, 0:32], ALU.mult)
        qr = qkr[t % 2]
        pzs = []
        for nb in range(6):
            pz = ps_z[nb % 4]
            for ci in range(8):
                k.mm(pz[:, :], hT[:, ci, :], win[:, ci, nb * 512:(nb + 1) * 512], start=(ci == 0), stop=(ci == 7))
            if nb < 4:
                k.act(sq[:, :], pz[:, :], AF.Square)
                k.op("dve", lambda: nc.vector.tensor_reduce(out=ssg.t[:, nb * 8:(nb + 1) * 8],
                                                             in_=sq.t[:, :].rearrange("p (g d) -> p g d", d=64),
                                                             axis=AX.X, op=ALU.add),
                     [ssg[:, nb * 8:(nb + 1) * 8]], [sq[:, :]])
                rs = ssg[:, nb * 8:(nb + 1) * 8]
                k.ts("dve", rs, rs, 1.0 / 64, 1e-6, ALU.mult, ALU.add)
                k.tt("pool", rs, rs, c.mhalf[:, 0:8], ALU.pow)
                z_ = zn[nb % 2]
                k.tt("dve", z_[:, :, :], V(pz.t[:, :].rearrange("p (g d) -> p g d", d=64), pz.trk),
                     V(ssg.t[:, nb * 8:(nb + 1) * 8].unsqueeze(2).to_broadcast([128, 8, 64]), ssg.trk), ALU.mult)
                w_ = nb // 2
                ta = tmpa[nb % 2]
                tb = tmpb[nb % 2]
                def bc(tile_, i0, i1):
                    return V(tile_.t[:, i0, i1, :].unsqueeze(1).to_broadcast([128, 8, 32]), tile_.trk)
                o1 = qr[:, (nb * 4):(nb * 4 + 4), :]
                ov = qr.t[:, nb * 4:(nb + 1) * 4, :].rearrange("p h (c d) -> p (h c) d", d=64)
                k.tt("dve", ta[:, :, :], z_[:, :, 0:32], bc(cg, w_, 0), ALU.mult)
                k.tt("pool", tb[:, :, :], z_[:, :, 32:64], bc(sg_, w_, 0), ALU.mult)
                k.tt("dve", V(ov[:, :, 0:32], qr.trk), ta[:, :, :], tb[:, :, :], ALU.subtract)
                ta2 = tmpa[(nb + 1) % 2]
                tb2 = tmpb[(nb + 1) % 2]
                k.tt("pool", ta2[:, :, :], z_[:, :, 32:64], bc(cg, w_, 1), ALU.mult)
                k.tt("dve", tb2[:, :, :], z_[:, :, 0:32], bc(sg_, w_, 1), ALU.mult)
                k.tt("pool", V(ov[:, :, 32:64], qr.trk), ta2[:, :, :], tb2[:, :, :], ALU.add)
            else:
                k.act(vt[t % 2][:, (nb - 4) * 512:(nb - 3) * 512], pz[:, :], AF.Copy)
        k.dma("act", v_d[t * 128:(t + 1) * 128, :], vt[t % 2][:, :])

    def a1_part2(t):
        qr = qkr[t % 2]
        qT = qkT[t % 2]
        for half in range(2):
            pb = ps_b[half]
            for j in range(8):
                k.tr(pb[:, j * 128:(j + 1) * 128], qr[:, half * 8 + j, :], identb[:, :])
            if half == 0:
                k.copy("dve", qT[:, 0:8, :], V(pb.t[:, :].rearrange("p (j s) -> p j s", s=128), pb.trk))
            else:
                k.copy("act", qT[:, 8:16, :], V(pb.t[:, :].rearrange("p (j s) -> p j s", s=128), pb.trk))
        k.dma("act", V(qkT_dv[:, :, t * 128:(t + 1) * 128], qkT_d.trk), qT[:, :, :])

    a1_part1(0)
    for t in range(NTT):
        if t + 1 < NTT:
            a1_part1(t + 1)
        a1_part2(t)

    ps_s = [ps_z[0], ps_z[1]]
    ps_acc = [[ps_t[0], ps_t[1]], [ps_z[2], ps_z[3]]]
    ps_o = ps_b[0]
    qTh = [k.sb(f"a_qTh{i}", [128, S], BF16) for i in range(2)]
    kTh = [[k.sb(f"a_kTh{i}_{cc}", [128, S], BF16) for cc in range(2)] for i in range(2)]
    for i in range(2):
        k.memset("pool", kTh[i][0][64:128, :], 0.0)
        k.memset("pool", kTh[i][1][0:64, :], 0.0)
    Vh = [k.sb(f"a_Vh{i}", [128, NTT, 132], BF16) for i in range(2)]
    for i in range(2):
        k.memset("pool", Vh[i][:, :, 128:129], 1.0)
    NPT = 4
    pT = [k.sb(f"a_pT{i}", [128, 512], BF16) for i in range(NPT)]
    o0 = [k.sb(f"a_o0{i}", [128, 4, 128], F32) for i in range(2)]
    oc = [k.sb(f"a_oc{i}", [128, 128], F32) for i in range(2)]
    osq = [k.sb(f"a_osq{i}", [128, 128], F32) for i in range(2)]
    NSLOT = 4
    onb = [k.sb(f"a_onb{i}", [128, 4, 128], BF16) for i in range(NSLOT)]
    rl = [k.sb(f"a_rl{i}", [128, 1], F32) for i in range(4)]
    sso = [k.sb(f"a_sso{i}", [128, 1], F32) for i in range(2)]
    oTs = [k.sb(f"a_oTs{i}", [128, 512], BF16) for i in range(2)]
    v_dv = v_d.t.rearrange("(kt p) f -> p kt f", p=128)

    def load_head(h):
        b = h % 2
        k.dma("sp", qTh[b][:, :], qkT_d[h, :, :])
        k.dma("sp", kTh[b][0][0:64, :], qkT_d[8 + h, 0:64, :])
        k.dma("sp", kTh[b][1][64:128, :], qkT_d[8 + h, 64:128, :])
        k.dma("sp", Vh[b][:, :, 0:128], V(v_dv[:, :, h * 128:(h + 1) * 128], v_d.trk))

    groups = [(h, qt, cc) for h in range(8) for qt in range(8) for cc in range(2)]
    pairs = []
    for gi, (h, qt, cc) in enumerate(groups):
        nk = 4 * qt + 4
        for kt in range(nk):
            pairs.append((gi, kt, kt == nk - 1))

    def emit_score(pi):
        gi, kt, _ = pairs[pi]
        h, qt, cc = groups[gi]
        b = h % 2
        r = kt * 128 - qt * 512
        j0 = max(r, 0)
        pss = ps_s[pi % 2]
        pt = pT[pi % NPT]
        k.mm(pss[:, j0:512], kTh[b][cc][:, kt * 128:(kt + 1) * 128],
             qTh[b][:, qt * 512 + j0:(qt + 1) * 512])
        k.act(pt[:, j0:512], pss[:, j0:512], AF.Exp, scale=ATT_SCALE)
        if r >= 0:
            k.tt("pool", pt[:, r:r + 128], pt[:, r:r + 128], tri[:, :], ALU.mult)

    def emit_pv(pi):
        gi, kt, _ = pairs[pi]
        h, qt, cc = groups[gi]
        b = h % 2
        r = kt * 128 - qt * 512
        acc = ps_acc[gi % 2]
        pt = pT[pi % NPT]
        for qs in range(4):
            if qs * 128 + 127 < r:
                continue
            last_kt = qt * 4 + qs
            k.mm(acc[qs // 2][:, (qs % 2) * 132:(qs % 2) * 132 + 129],
                 pt[:, qs * 128:(qs + 1) * 128], Vh[b][:, kt, 0:129],
                 start=(kt == 0 and qs % 2 == 0), stop=(kt == last_kt), skip_group_check=True)

    def emit_epi_a(gi, slot):
        h, qt, cc = groups[gi]
        acc = ps_acc[gi % 2]
        for qs in range(4):
            a_ = acc[qs // 2]
            base = (qs % 2) * 132
            r_ = rl[qs]
            k.recip(r_[:, :], a_[:, base + 128:base + 129])
            if cc == 0:
                k.ts("dve", o0[qt % 2][:, qs, :], a_[:, base:base + 128], r_[:, 0:1], None, ALU.mult)
            else:
                oc_ = oc[qs % 2]
                ss_ = sso[qs % 2]
                k.tt("dve", r_[:, :], r_[:, :], nlam[:, :], ALU.mult)
                k.stt(oc_[:, :], a_[:, base:base + 128], r_[:, 0:1], o0[qt % 2][:, qs, :], ALU.mult, ALU.add)
                k.tt("pool", osq[qs % 2][:, :], oc_[:, :], oc_[:, :], ALU.mult)
                k.op("dve", lambda: nc.vector.tensor_reduce(out=ss_.t[:, :], in_=osq[qs % 2].t[:, :], axis=AX.X, op=ALU.add),
                     [ss_[:, :]], [osq[qs % 2][:, :]])
                k.ts("dve", ss_[:, :], ss_[:, :], 1.0 / 128, 1e-5, ALU.mult, ALU.add)
                k.tt("pool", ss_[:, :], ss_[:, :], c.mhalf[:, 0:1], ALU.pow)
                k.stt(onb[slot][:, qs, :], oc_[:, :], ss_[:, 0:1], gsub[:, :], ALU.mult, ALU.mult)

    def emit_epi_b(gi, slot):
        h, qt, cc = groups[gi]
        for qs in range(4):
            k.tr(ps_o[:, qs * 128:(qs + 1) * 128], onb[slot][:, qs, :], identb[:, :])
        ot = oTs[qt % 2]
        k.copy("dve", ot[:, :], ps_o[:, 0:512])
        k.dma("sp", oT_d[h, :, qt * 512:(qt + 1) * 512], ot[:, :])

    LA = 2
    DEFER = 16
    deferred = []
    nslot = 0
    load_head(0)
    cur_h = -1
    npairs = len(pairs)
    for i in range(npairs + LA):
        if i < npairs:
            emit_score(i)
        j = i - LA
        if j >= 0:
            h = groups[pairs[j][0]][0]
            if h != cur_h:
                cur_h = h
                if h + 1 < 8:
                    load_head(h + 1)
            emit_pv(j)
            gi, kt, is_last = pairs[j]
            if is_last:
                if groups[gi][2] == 1:
                    slot = nslot % NSLOT
                    nslot += 1
                    for d_ in [d_ for d_ in deferred if d_[2] == slot]:
                        deferred.remove(d_)
                        emit_epi_b(d_[1], d_[2])
                    emit_epi_a(gi, slot)
                    deferred.append((j + DEFER, gi, slot))
                else:
                    emit_epi_a(gi, None)
            while deferred and deferred[0][0] <= j:
                d_ = deferred.pop(0)
                emit_epi_b(d_[1], d_[2])
    while deferred:
        d_ = deferred.pop(0)
        emit_epi_b(d_[1], d_[2])

    oTt = [k.sb(f"a_oTt{i}", [128, 8, 128], BF16) for i in range(2)]
    oT_dv = oT_d.t.rearrange("h p s -> p h s")
    for t in range(NTT):
        xs = xs2[t % 2]
        ot = oTt[t % 2]
        k.dma("sp", xs[:, :], xin[t * 128:(t + 1) * 128, :])
        k.dma("sp", ot[:, :, :], V(oT_dv[:, :, t * 128:(t + 1) * 128], oT_d.trk))
        for dh in range(2):
            po = ps_z[dh]
            for hh in range(8):
                k.mm(po[:, :], ot[:, hh, :], wout[:, hh, dh * 512:(dh + 1) * 512], start=(hh == 0), stop=(hh == 7))
            k.tt("dve", xs[:, dh * 512:(dh + 1) * 512], po[:, :], xs[:, dh * 512:(dh + 1) * 512], ALU.add)
        k.dma("pool", xout[t * 128:(t + 1) * 128, :], xs[:, :])
    k.pop_scope()

GN_EPS_ = 64e-5
NEG_E05 = -math.exp(-0.5)


def col_load(k, c, name, ap1d, ncol):
    t = k.sb(name, [128, ncol], F32)
    k.dma("sp", t[:, :], V(ap1d.rearrange("(c p) -> p c", p=128), c.wtrk), allow_slow_non_contiguous=True)
    return t


def norm_tile_to_hT(k, c, xs, xn, ss, ps_pair, hT, s, gcol):
    rmsnorm_tile(k, c, xs[:, :], xn[:, :], ss[:, :], D, 1e-6)
    for half in range(2):
        pb = ps_pair[half]
        for j in range(4):
            ci = half * 4 + j
            k.tr(pb[:, j * 128:(j + 1) * 128], xn[:, ci * 128:(ci + 1) * 128], c.ident[:, :])
        for j in range(4):
            ci = half * 4 + j
            if j % 2 == 0:
                k.ts("dve", hT[:, ci, s * 128:(s + 1) * 128], pb[:, j * 128:(j + 1) * 128], gcol[:, ci:ci + 1], None, ALU.mult)
            else:
                k.act(hT[:, ci, s * 128:(s + 1) * 128], pb[:, j * 128:(j + 1) * 128], AF.Copy, scale=gcol[:, ci:ci + 1])


def mix0_pass2(k, c, xin, xout, I, yaT_d, use_ya=True):
    nc = k.nc
    k.push_scope()
    W = c.wtrk
    ident = c.ident
    P = [k.ps(f"m2_ps{i}") for i in range(8)]
    winu = k.sb("m2_winu", [128, 8, 1024], BF16)
    wout = k.sb("m2_wout", [128, 8, 1024], BF16)
    winv = I["a_w_in"][0].rearrange("(c p) f -> p c f", p=128)
    woutv = I["e_w_out"][0].rearrange("(c p) f -> p c f", p=128)
    for ci in range(8):
        k.dma("pool", winu[:, ci, :], V(winv[:, ci, 1792:2816], W))
    for ci in range(8):
        k.dma("pool", wout[:, ci, :], V(woutv[:, ci, :], W))
    gcol = col_load(k, c, "m2_gcol", I["mix_norm"][0], 8)
    gb = col_load(k, c, "m2_gb", I["b_glu_bias"][0], 8)
    dwb = col_load(k, c, "m2_dwb", I["b_dw_bias"][0], 4)
    clw = col_load(k, c, "m2_clw", I["b_ln_w"][0], 4)
    clb = col_load(k, c, "m2_clb", I["b_ln_b"][0], 4)
    dwr = k.sb("m2_dwr", [31, 512], F32)
    k.dma("sp", dwr[:, :], V(I["b_dw"][0], W))
    dwT = k.sb("m2_dwT", [128, 4, 32], F32)
    for i in range(4):
        k.tr(P[0][:, i * 32:i * 32 + 31], dwr[0:31, i * 128:(i + 1) * 128], ident[0:31, 0:31])
    k.copy("dve", dwT[:, :, :], V(P[0].t[:, 0:128].rearrange("p (i j) -> p i j", j=32), P[0].trk))
    Dg = k.sb("m2_Dg", [128, 4, 31, 128], BF16)
    n = 0
    for i in range(4):
        for j in range(31):
            k.ts("dve" if n % 2 == 0 else "pool", Dg[:, i, j, :], ident[:, :], dwT[:, i, j:j + 1], None, ALU.mult)
            n += 1
    ones512 = k.sb("m2_ones", [128, 128], F32)
    k.memset("pool", ones512[:, :], 1.0 / 512)
    eps5 = k.sb("m2_eps", [128, 1], F32)
    k.memset("pool", eps5[:, :], 1e-5)
    glT = [k.sb(f"m2_glT{p}", [128, 4, 542], BF16) for p in range(2)]
    k.memset("pool", glT[0][:, :, 0:30], 0.0)
    xs4 = [[k.sb(f"m2_xs{p}_{i}", [128, D], F32) for i in range(4)] for p in range(2)]
    xn = k.sb("m2_xn", [128, D], F32)
    ssn = k.sb("m2_ssn", [128, 1], F32)
    hT = [k.sb(f"m2_hT{p}", [128, 8, 512], BF16) for p in range(2)]
    sig = [k.sb(f"m2_sig{i}", [128, 512], F32) for i in range(2)]
    cb = [k.sb(f"m2_cb{i}", [128, 512], F32) for i in range(4)]
    dd = [k.sb(f"m2_dd{i}", [128, 512], F32) for i in range(4)]
    dsq = [k.sb(f"m2_dsq{i}", [128, 512], F32) for i in range(2)]
    rstd = k.sb("m2_rstd", [128, 512], F32)
    yT = [k.sb(f"m2_yT{p}", [128, 8, 512], BF16) for p in range(2)]
    if not use_ya:
        for p in range(2):
            k.memset("pool", yT[p][:, 0:4, :], 0.0)
    yaT_dv = yaT_d.t.rearrange("h p s -> p h s") if use_ya else None
    NT2 = S // 512

    def S1(ti):
        p = ti % 2
        for s in range(4):
            r0 = ti * 512 + s * 128
            k.dma("sp", xs4[p][s][:, :], xin[r0:r0 + 128, :])
            norm_tile_to_hT(k, c, xs4[p][s], xn, ssn, (P[0], P[1]), hT[p], s, gcol)
        if use_ya:
            k.dma("sp", yT[p][:, 0:4, :], V(yaT_dv[:, :, ti * 512:(ti + 1) * 512], yaT_d.trk))
        if ti > 0:
            for i in range(4):
                k.copy("pool", glT[p][:, i, 0:30], glT[1 - p][:, i, 512:542])
        for i in range(4):
            pa, pb_ = P[2], P[3]
            for ci in range(8):
                k.mm(pa[:, :], winu[:, ci, i * 128:(i + 1) * 128], hT[p][:, ci, :], start=(ci == 0), stop=(ci == 7))
            for ci in range(8):
                k.mm(pb_[:, :], winu[:, ci, (4 + i) * 128:(5 + i) * 128], hT[p][:, ci, :], start=(ci == 0), stop=(ci == 7))
            sg = sig[i % 2]
            k.act(sg[:, :], pb_[:, :], AF.Sigmoid, bias=gb[:, 4 + i:5 + i])
            k.stt(glT[p][:, i, 30:542], pa[:, :], gb[:, i:i + 1], sg[:, :], ALU.add, ALU.mult)

    def S2(ti):
        p = ti % 2
        for i in range(4):
            pc = P[4 + (i % 2)]
            for j in range(31):
                k.mm(pc[:, :], Dg[:, i, j, :], glT[p][:, i, j:j + 512], start=(j == 0), stop=(j == 30))
            k.ts("dve", cb[i][:, :], pc[:, :], dwb[:, i:i + 1], None, ALU.add)
        pm = P[6]
        for i in range(4):
            k.mm(pm[:, :], ones512[:, :], cb[i][:, :], start=(i == 0), stop=(i == 3))
        for i in range(4):
            k.tt("dve", dd[i][:, :], cb[i][:, :], pm[:, :], ALU.subtract)
        pv = P[7]
        for i in range(4):
            q_ = dsq[i % 2]
            k.act(q_[:, :], dd[i][:, :], AF.Square)
            k.mm(pv[:, :], ones512[:, :], q_[:, :], start=(i == 0), stop=(i == 3))
        k.act(rstd[:, :], pv[:, :], AF.Sqrt, bias=eps5[:, 0:1])
        k.recip(rstd[:, :], rstd[:, :])
        for i in range(4):
            k.tt("dve" if i % 2 == 0 else "pool", dd[i][:, :], dd[i][:, :], rstd[:, :], ALU.mult)
            k.act(yT[p][:, 4 + i, :], dd[i][:, :], AF.Silu, scale=clw[:, i:i + 1], bias=clb[:, i:i + 1])
        for s in range(4):
            xs = xs4[p][s]
            for dh in range(2):
                po = P[4 + dh]
                for fc in range(8):
                    k.mm(po[:, :], yT[p][:, fc, s * 128:(s + 1) * 128], wout[:, fc, dh * 512:(dh + 1) * 512],
                         start=(fc == 0), stop=(fc == 7))
                k.tt("dve", xs[:, dh * 512:(dh + 1) * 512], po[:, :], xs[:, dh * 512:(dh + 1) * 512], ALU.add)
            r0 = ti * 512 + s * 128
            k.dma("pool", xout[r0:r0 + 128, :], xs[:, :])

    S1(0)
    for ti in range(NT2):
        if ti + 1 < NT2:
            S1(ti + 1)
        S2(ti)
    k.pop_scope()


class _Cut(Exception):
    pass


def mix0_pass1(k, c, xin, I, yaT_d, ntiles=S // 512, dbg=None, cut=None):
    nc = k.nc
    k.push_scope()
    W = c.wtrk
    ident = c.ident
    P = [k.ps(f"m1_ps{i}") for i in range(7)]
    PB = k.ps("m1_psb", (128, 1024), BF16)
    identb = k.sb("m1_identb", [128, 128], BF16)
    k.copy("dve", identb[:, :], ident[:, :])
    win = k.sb("m1_win", [128, 8, 1792], BF16)
    winv = I["a_w_in"][0].rearrange("(c p) f -> p c f", p=128)
    for ci in range(8):
        k.dma("pool", win[:, ci, :], V(winv[:, ci, 0:1792], W))
    w2a2 = k.sb("m1_w2a2", [128, 512], BF16)
    k.dma("pool", w2a2[0:64, :], V(I["a_w2"][0], W))
    k.dma("pool", w2a2[64:128, :], V(I["a_a2"][0], W))
    g2b = k.sb("m1_g2b", [128, 512], BF16)
    k.dma("pool", g2b[:, :], V(I["a_g2"][0], W))
    gcol = col_load(k, c, "m1_gcol", I["mix_norm"][0], 8)
    mu = col_load(k, c, "m1_mu", I["a_mu"][0], 14)
    omu = k.sb("m1_omu", [128, 14], F32)
    k.ts("dve", omu[:, :], mu[:, :], -1.0, 1.0, ALU.mult, ALU.add)
    w0c = col_load(k, c, "m1_w0", I["a_w0"][0], 4)
    a0c = col_load(k, c, "m1_a0", I["a_a0"][0], 4)
    kkc = col_load(k, c, "m1_kk", I["a_k_k"][0], 4)
    kac = col_load(k, c, "m1_ka", I["a_k_a"][0], 4)
    okac = k.sb("m1_oka", [128, 4], F32)
    k.ts("dve", okac[:, :], kac[:, :], -1.0, 1.0, ALU.mult, ALU.add)
    rkc = col_load(k, c, "m1_rk", I["a_r_k"][0].rearrange("h d -> (h d)"), 4)
    lnw = col_load(k, c, "m1_lnw", I["a_ln_w"][0], 4)
    lnb = col_load(k, c, "m1_lnb", I["a_ln_b"][0], 4)
    bones = k.sb("m1_bones", [128, 128], F32)
    k.memset("pool", bones[:, :], 0.0)
    k.memset("pool", bones[0:64, 0:64], 1.0)
    k.memset("pool", bones[64:128, 64:128], 1.0)
    bones64 = k.sb("m1_bones64", [128, 128], F32)
    k.ts("dve", bones64[:, :], bones[:, :], 1.0 / 64, None, ALU.mult)
    onesT = k.sb("m1_onesT", [128, 128], F32)
    k.memset("pool", onesT[:, :], 1.0)
    epsg = k.sb("m1_epsg", [128, 1], F32)
    k.memset("pool", epsg[:, :], GN_EPS_)
    MSK = k.sb("m1_MSK", [128, 512], F32)
    MSKL = k.sb("m1_MSKL", [128, 512], F32)
    k.memset("pool", MSK[:, :], 1.0)
    k.memset("pool", MSKL[:, :], 1.0)
    for q in range(4):
        strict = (q % 2 == 0)
        blk = MSK[:, q * 128:(q + 1) * 128]
        k.op("pool", lambda: nc.gpsimd.affine_select(blk.ap, blk.ap, pattern=[[1, 128]], compare_op=ALU.is_ge,
                                                      fill=0.0, base=(-1 if strict else 0), channel_multiplier=-1),
             [blk], [blk])
        blk2 = MSKL[:, q * 128:(q + 1) * 128]
        k.op("pool", lambda: nc.gpsimd.affine_select(blk2.ap, blk2.ap, pattern=[[-1, 128]], compare_op=ALU.is_ge,
                                                      fill=0.0, base=-1, channel_multiplier=1),
             [blk2], [blk2])
    zlast = k.sb("m1_zlast", [128, 14], F32)
    k.memset("pool", zlast[:, :], 0.0)
    S2 = [k.sb(f"m1_S2{i}", [128, 4, 128], F32) for i in range(2)]
    k.memset("pool", S2[0][:, :, :], 0.0)
    k.memset("pool", S2[1][:, :, :], 0.0)
    PhiT2 = k.sb("m1_PhiT2", [128, 4, 128], F32)
    k.memset("pool", PhiT2[:, :, :], 0.0)
    Psi = k.sb("m1_Psi", [128, 4, 128], F32)
    k.memset("pool", Psi[:, :, :], 0.0)
    xs2 = [k.sb(f"m1_xs{i}", [128, D], F32) for i in range(2)]
    xn = k.sb("m1_xn", [128, D], F32)
    ssn = k.sb("m1_ssn", [128, 1], F32)
    hT = k.sb("m1_hT", [128, 8, 512], BF16)
    zraw = [k.sb(f"m1_zraw{i}", [128, 513], F32) for i in range(2)]
    ztmp = [k.sb(f"m1_ztmp{i}", [128, 512], F32) for i in range(2)]
    zs12 = k.sb("m1_zs12", [128, 512], F32)
    zs13 = k.sb("m1_zs13", [128, 512], F32)
    c12b = k.sb("m1_c12b", [128, 512], BF16)
    sgd = k.sb("m1_sgd", [128, 512], BF16)
    def f32t(name):
        return k.sb("m1f_" + name, [128, 512], F32)
    r_t, k_t, v_t = f32t("r"), f32t("k"), f32t("v")
    lw, a_t, kk, kk2, kkn, ka, kmod, Bq, rkr, cl, Pinc, Pexc, Pinv, E2 = [f32t(n) for n in
        ["lw", "a", "kk", "kk2", "kkn", "ka", "kmod", "Bq", "rkr", "cl", "Pinc", "Pexc", "Pinv", "E2"]]
    gT = [f32t(f"gT{i}") for i in range(4)]
    bon = [f32t(f"bon{i}") for i in range(4)]
    YT = [f32t(f"YT{i}") for i in range(4)]
    PC = k.sb("m1_PC", [128, 4, 4], F32)
    ART = [k.sb(f"m1_ART{i}", [128, 2, 512], BF16) for i in range(4)]
    def b16t(name):
        return [k.sb(f"m1_{name}{i}", [128, 512], BF16) for i in range(4)]
    BT, KT, BPT, KPT, vTb = b16t("BT"), b16t("KT"), b16t("BPT"), b16t("KPT"), b16t("vTb")
    MM = k.sb("m1_MM", [128, 8, 512], BF16)
    X = [k.sb(f"m1_X{g}", [128, 4, 128], BF16) for g in range(2)]
    NTb = [[k.sb(f"m1_NT{i}_{g}", [128, 4, 128], BF16) for g in range(2)] for i in range(2)]
    Nb = [[k.sb(f"m1_N{i}_{g}", [128, 4, 128], BF16) for g in range(2)] for i in range(2)]
    Vt = k.sb("m1_Vt", [128, 4, 128], BF16)
    BPt = k.sb("m1_BPt", [128, 4, 128], BF16)
    KPt = k.sb("m1_KPt", [128, 4, 128], BF16)
    RhT = k.sb("m1_RhT", [128, 4, 128], F32)
    YhT = k.sb("m1_YhT", [128, 4, 128], F32)
    dtl = f32t("dtl")
    dsq = f32t("dsq")
    rsd = f32t("rsd")
    yaT = [k.sb(f"m1_yaT{i}", [128, 512], BF16) for i in range(2)]
    ssel = [0]
    rot = [0]

    def proj(chunk, pz):
        for ci in range(8):
            k.mm(pz[:, :], win[:, ci, chunk * 128:(chunk + 1) * 128], hT[:, ci, :], start=(ci == 0), stop=(ci == 7))

    def shiftmix(chunk, pz, out_tile):
        zr = zraw[rot[0] % 2]
        zt = ztmp[rot[0] % 2]
        rot[0] += 1
        k.copy("act", zr[:, 1:513], pz[:, :])
        k.copy("pool", zr[:, 0:1], zlast[:, chunk:chunk + 1])
        k.ts("dve", zt[:, :], pz[:, :], omu[:, chunk:chunk + 1], None, ALU.mult)
        k.stt(out_tile[:, :], zr[:, 0:512], mu[:, chunk:chunk + 1], zt[:, :], ALU.mult, ALU.add)
        k.copy("pool", zlast[:, chunk:chunk + 1], zr[:, 512:513])

    def cutpt(n):
        if cut == n:
            raise _Cut()

    try:
      for ti in range(ntiles):
          for s in range(4):
              xs = xs2[s % 2]
              r0 = ti * 512 + s * 128
              k.dma("sp", xs[:, :], xin[r0:r0 + 128, :])
              norm_tile_to_hT(k, c, xs, xn, ssn, (P[0], P[1]), hT, s, gcol)
          cutpt('a')
          proj(12, P[2])
          shiftmix(12, P[2], zs12)
          proj(13, P[3])
          shiftmix(13, P[3], zs13)
          k.act(c12b[0:64, :], zs12[0:64, :], AF.Tanh)
          k.copy("pool", c12b[64:128, :], zs12[64:128, :])
          k.act(sgd[:, :], zs13[:, :], AF.Sigmoid)
          cutpt('b')
          for hp in range(4):
              proj(hp, P[2])
              shiftmix(hp, P[2], r_t)
              proj(4 + hp, P[3])
              shiftmix(4 + hp, P[3], k_t)
              proj(8 + hp, P[2])
              shiftmix(8 + hp, P[2], v_t)
              k.copy("pool", vTb[hp][:, :], v_t[:, :])
              hs = slice(hp * 128, (hp + 1) * 128)
              k.mm(P[4][:, :], w2a2[0:64, hs], c12b[0:64, :])
              k.act(lw[:, :], P[4][:, :], AF.Sigmoid, bias=w0c[:, hp:hp + 1])
              k.ts("dve", lw[:, :], lw[:, :], NEG_E05, None, ALU.mult)
              k.mm(P[5][:, :], w2a2[64:128, hs], c12b[64:128, :])
              k.act(a_t[:, :], P[5][:, :], AF.Sigmoid, bias=a0c[:, hp:hp + 1])
              k.mm(P[6][:, :], g2b[:, hs], sgd[:, :])
              k.copy("act", gT[hp][:, :], P[6][:, :])
              k.ts("dve", kk[:, :], k_t[:, :], kkc[:, hp:hp + 1], None, ALU.mult)
              k.tt("pool", kk2[:, :], kk[:, :], kk[:, :], ALU.mult)
              k.mm(P[4][:, :], bones[:, :], kk2[:, :])
              k.act(kk2[:, :], P[4][:, :], AF.Sqrt)
              k.ts("dve", kk2[:, :], kk2[:, :], 1e-12, None, ALU.max)
              k.recip(kk2[:, :], kk2[:, :])
              k.tt("dve", kkn[:, :], kk[:, :], kk2[:, :], ALU.mult)
              k.ts("dve", ka[:, :], a_t[:, :], kac[:, hp:hp + 1], okac[:, hp:hp + 1], ALU.mult, ALU.add)
              k.tt("pool", kmod[:, :], k_t[:, :], ka[:, :], ALU.mult)
              k.tt("pool", Bq[:, :], kkn[:, :], a_t[:, :], ALU.mult)
              k.stt(rkr[:, :], r_t[:, :], rkc[:, hp:hp + 1], kmod[:, :], ALU.mult, ALU.mult)
              k.mm(P[5][:, :], bones[:, :], rkr[:, :])
              k.tt("dve", bon[hp][:, :], P[5][:, :], v_t[:, :], ALU.mult)
              for cc in range(4):
                  cs = slice(cc * 128, (cc + 1) * 128)
                  k.op("dve", lambda: nc.vector.tensor_tensor_scan(cl.t[:, cs], onesT.t[:, :], lw.t[:, cs], 0.0,
                                                                   ALU.mult, ALU.add),
                       [cl[:, cs]], [onesT[:, :], lw[:, cs]])
              k.act(Pinc[:, :], cl[:, :], AF.Exp)
              k.tt("pool", Pexc[:, :], cl[:, :], lw[:, :], ALU.subtract)
              k.act(Pexc[:, :], Pexc[:, :], AF.Exp)
              k.act(Pinv[:, :], cl[:, :], AF.Exp, scale=-1.0)
              for cc in range(4):
                  cs = slice(cc * 128, (cc + 1) * 128)
                  k.act(E2[:, cs], cl[:, cs], AF.Exp, scale=-1.0, bias=cl[:, cc * 128 + 127:cc * 128 + 128])
              k.copy("pool", PC[:, hp, :], V(Pinc.t[:, :].rearrange("p (c t) -> p c t", t=128)[:, :, 127], Pinc.trk))
              k.stt(ART[hp][:, 0, :], kkn[:, :], -1.0, Pexc[:, :], ALU.mult, ALU.mult)
              k.tt("pool", ART[hp][:, 1, :], r_t[:, :], Pinc[:, :], ALU.mult)
              k.tt("dve", BT[hp][:, :], Bq[:, :], Pinv[:, :], ALU.mult)
              k.tt("pool", KT[hp][:, :], kmod[:, :], Pinv[:, :], ALU.mult)
              k.tt("dve", BPT[hp][:, :], Bq[:, :], E2[:, :], ALU.mult)
              k.tt("pool", KPT[hp][:, :], kmod[:, :], E2[:, :], ALU.mult)
          cutpt('c')
          for cc in range(4):
              cs = slice(cc * 128, (cc + 1) * 128)
              for hp in range(4):
                  k.tr(PB[:, hp * 128:(hp + 1) * 128], ART[hp][:, 0, cs], identb[:, :])
                  k.tr(PB[:, 512 + hp * 128:512 + (hp + 1) * 128], vTb[hp][:, cs], identb[:, :])
              cutpt('d0a')
              for g in range(2):
                  k.copy("dve", X[g][:, :, 0:64],
                         V(PB.t[:, g * 256:(g + 1) * 256].rearrange("p (h d) -> p h d", d=64), PB.trk))
              cutpt('d0b')
              k.copy("act", Vt[:, :, :], V(PB.t[:, 512:1024].rearrange("p (h d) -> p h d", d=128), PB.trk))
              cutpt('d0c')
              for hp in range(4):
                  k.tr(PB[:, hp * 128:(hp + 1) * 128], BPT[hp][:, cs], identb[:, :])
                  k.tr(PB[:, 512 + hp * 128:512 + (hp + 1) * 128], KPT[hp][:, cs], identb[:, :])
              cutpt('d0d')
              k.copy("dve", BPt[:, :, :], V(PB.t[:, 0:512].rearrange("p (h d) -> p h d", d=128), PB.trk))
              cutpt('d0e')
              k.copy("act", KPt[:, :, :], V(PB.t[:, 512:1024].rearrange("p (h d) -> p h d", d=128), PB.trk))
              cutpt('d1')
              for h in range(8):
                  hp, e = h // 2, h % 2
                  es_ = slice(e * 64, (e + 1) * 64)
                  pm = P[h % 2]
                  k.mm(pm[:, 0:128], BT[hp][es_, cs], ART[hp][es_, 0, cs])
                  k.mm(pm[:, 128:256], BT[hp][es_, cs], ART[hp][es_, 1, cs])
                  k.mm(pm[:, 256:384], KT[hp][es_, cs], ART[hp][es_, 0, cs])
                  k.mm(pm[:, 384:512], KT[hp][es_, cs], ART[hp][es_, 1, cs])
                  k.tt("dve", MM[:, h, :], pm[:, :], MSK[:, :], ALU.mult)
                  pn = P[2 + e]
                  k.mm(pn[:, hp * 128:(hp + 1) * 128], ART[hp][es_, 0, cs], BT[hp][es_, cs])
              for e in range(2):
                  for g in range(2):
                      k.tt("dve", V(Nb[0][g].t[:, :, :].rearrange("p (q e) s -> p q e s", e=2)[:, :, e, :], Nb[0][g].trk),
                           V(P[2 + e].t[:, g * 256:(g + 1) * 256].rearrange("p (h s) -> p h s", s=128), P[2 + e].trk),
                           V(MSKL.t[:, 0:256].rearrange("p (h s) -> p h s", s=128), MSKL.trk), ALU.mult)
              cutpt('d2')
              for h in range(8):
                  hp, e = h // 2, h % 2
                  k.mm(P[4][:, h * 64:(h + 1) * 64], MM[:, h, 256:384], Vt[:, hp, e * 64:(e + 1) * 64])
              for g in range(2):
                  k.copy("act", X[g][:, :, 64:128],
                         V(P[4].t[:, g * 256:(g + 1) * 256].rearrange("p (h d) -> p h d", d=64), P[4].trk))
              cutpt('d3')
              def lvl_mm(j, g):
                  ntv = (lambda h: MM[:, h, 0:128]) if j == 0 else (lambda h, t_=NTb[j % 2][g]: t_[:, h % 4, :])
                  nv = (lambda h, t_=Nb[j % 2][g]: t_[:, h % 4, :])
                  hs_ = range(4 * g, 4 * g + 4)
                  for h in hs_:
                      k.mm(P[5 + g][:, (h % 4) * 128:(h % 4 + 1) * 128], ntv(h), X[g][:, h % 4, :])
                  if j < 6:
                      for h in hs_:
                          k.mm(P[2 + g][:, (h % 4) * 128:(h % 4 + 1) * 128], nv(h), ntv(h))
                  if j < 5:
                      for h in hs_:
                          k.mm(P[g][:, (h % 4) * 128:(h % 4 + 1) * 128], ntv(h), nv(h))

              def lvl_evac(j, g):
                  def pv_(b):
                      return V(b.t[:, :].rearrange("p (h s) -> p h s", s=128), b.trk)
                  k.tt("dve", X[g][:, :, :], pv_(P[5 + g]), X[g][:, :, :], ALU.add)
                  if j < 6:
                      k.copy("act", NTb[(j + 1) % 2][g][:, :, :], pv_(P[2 + g]))
                  if j < 5:
                      k.copy("act" if g == 0 else "dve", Nb[(j + 1) % 2][g][:, :, :], pv_(P[g]))

              for j in range(7):
                  lvl_mm(j, 0)
                  lvl_mm(j, 1)
                  lvl_evac(j, 0)
                  lvl_evac(j, 1)
              cutpt('d4')
              for h in range(8):
                  hp, e = h // 2, h % 2
                  es_ = slice(e * 64, (e + 1) * 64)
                  k.mm(P[0][es_, hp * 128:(hp + 1) * 128], X[h // 4][:, h % 4, 0:64], MM[:, h, 128:256])
                  k.mm(P[1][es_, hp * 128:(hp + 1) * 128], X[h // 4][:, h % 4, 64:128], MM[:, h, 128:256], start=True, stop=False,
                       skip_group_check=True)
                  k.mm(P[1][es_, hp * 128:(hp + 1) * 128], Vt[:, hp, es_], MM[:, h, 384:512], start=False, stop=True,
                       skip_group_check=True)
                  k.mm(P[4][es_, hp * 64:(hp + 1) * 64], X[h // 4][:, h % 4, 0:64], BPt[:, hp, es_])
                  k.mm(P[4][es_, 256 + hp * 64:256 + (hp + 1) * 64], BPt[:, hp, es_], X[h // 4][:, h % 4, 64:128], start=True, stop=False,
                       skip_group_check=True)
                  k.mm(P[4][es_, 256 + hp * 64:256 + (hp + 1) * 64], KPt[:, hp, es_], Vt[:, hp, es_], start=False, stop=True,
                       skip_group_check=True)
              for hp in range(4):
                  k.tt("dve", RhT[:, hp, :], P[0][:, hp * 128:(hp + 1) * 128], V(ART[hp].t[:, 1, cs], ART[hp].trk), ALU.add)
              k.copy("act", YhT[:, :, :], V(P[1].t[:, :].rearrange("p (h s) -> p h s", s=128), P[1].trk))
              for hp in range(4):
                  for e in range(2):
                      es_ = slice(e * 64, (e + 1) * 64)
                      k.stt(PhiT2[es_, hp, es_], ident[es_, es_], PC[es_, hp, cc:cc + 1],
                            P[4][es_, hp * 64:(hp + 1) * 64], ALU.mult, ALU.add)
                      k.copy("act", Psi[es_, hp, es_], P[4][es_, 256 + hp * 64:256 + (hp + 1) * 64])
              cutpt('d5')
              Scur = S2[ssel[0] % 2]
              Snew = S2[(ssel[0] + 1) % 2]
              ssel[0] += 1
              for hp in range(4):
                  k.mm(P[5][:, hp * 128:(hp + 1) * 128], Scur[:, hp, :], RhT[:, hp, :])
              for hp in range(4):
                  k.mm(P[6][:, hp * 128:(hp + 1) * 128], PhiT2[:, hp, :], Scur[:, hp, :])
              for hp in range(4):
                  k.tt("dve", YT[hp][:, cs], P[5][:, hp * 128:(hp + 1) * 128], YhT[:, hp, :], ALU.add)
              k.tt("dve", Snew[:, :, :], V(P[6].t[:, :].rearrange("p (h s) -> p h s", s=128), P[6].trk), Psi[:, :, :], ALU.add)
          cutpt('d6')
          for hp in range(4):
              y = YT[hp]
              if dbg is not None:
                  k.dma("sp", dbg[hp, :, ti * 512:(ti + 1) * 512], y[:, :])
              k.mm(P[0][:, :], bones64[:, :], y[:, :])
              k.tt("dve", dtl[:, :], y[:, :], P[0][:, :], ALU.subtract)
              k.act(dsq[:, :], dtl[:, :], AF.Square)
              k.mm(P[1][:, :], bones64[:, :], dsq[:, :])
              k.act(rsd[:, :], P[1][:, :], AF.Sqrt, bias=epsg[:, 0:1])
              k.recip(rsd[:, :], rsd[:, :])
              k.tt("pool", dtl[:, :], dtl[:, :], rsd[:, :], ALU.mult)
              k.stt(dsq[:, :], dtl[:, :], lnw[:, hp:hp + 1], bon[hp][:, :], ALU.mult, ALU.add)
              ya = yaT[hp % 2]
              k.stt(ya[:, :], dsq[:, :], lnb[:, hp:hp + 1], gT[hp][:, :], ALU.add, ALU.mult)
              k.dma("sp", yaT_d[hp, :, ti * 512:(ti + 1) * 512], ya[:, :])
    except _Cut:
        pass
    k.pop_scope()


def mix0_layer(k, c, xin, xout, I):
    yaT_d = k.dram("yaT_d", [4, 128, S], BF16)
    mix0_pass1(k, c, xin, I, yaT_d)
    mix0_pass2(k, c, xin, xout, I, yaT_d)


def rope_tables_np():
    inv = (1.0 / (np.float32(10000.0) ** (np.arange(0, 64, 2, dtype=np.float32) / np.float32(64)))).astype(np.float32)
    ang = (np.arange(S, dtype=np.float32)[:, None] * inv[None, :]).astype(np.float32)
    return np.cos(ang).astype(np.float32), np.sin(ang).astype(np.float32)


EXT_SHAPES = {
    "x": [S, D], "ffn_norm": [2, 2, D], "ffn_w_gate": [2, 2, D, DFF], "ffn_w_up": [2, 2, D, DFF],
    "ffn_w_down": [2, 2, DFF, D], "mix_norm": [2, D],
    "a_w_in": [1, D, 2816], "a_mu": [1, 1792], "a_w0": [1, 512], "a_w2": [1, 64, 512], "a_a0": [1, 512],
    "a_a2": [1, 64, 512], "a_g2": [1, 128, 512], "a_k_k": [1, 512], "a_k_a": [1, 512], "a_r_k": [1, 8, 64],
    "a_ln_w": [1, 512], "a_ln_b": [1, 512], "b_glu_bias": [1, 1024], "b_dw": [1, 31, 512], "b_dw_bias": [1, 512],
    "b_ln_w": [1, 512], "b_ln_b": [1, 512], "e_w_out": [1, D, D],
    "c_w_in": [1, D, 3 * D], "c_q_norm": [1, 64], "c_k_norm": [1, 64], "c_lq1": [1, 64], "c_lk1": [1, 64],
    "c_lq2": [1, 64], "c_lk2": [1, 64], "c_sub_norm": [1, 128], "c_w_out": [1, D, D],
    "rope_cos": [S, 32], "rope_sin": [S, 32],
}
ALL_PHASES = ("f00", "mix0", "f01", "f10", "attn", "f11")


def build_program(phases=ALL_PHASES):
    nc = bass.Bass("TRN2", target_bir_lowering=False)
    es = ExitStack()
    c = Ctx()
    with es:
        k = KB(nc, es)
        c.wtrk = Trk("weights")
        I = {}
        for name, shape in EXT_SHAPES.items():
            I[name] = nc.dram_tensor(name, list(shape), F32, kind="ExternalInput").ap()
        out = Buf(nc.dram_tensor("out", [S, D], F32, kind="ExternalOutput").ap(), "out")
        cur = Buf(I["x"], "x")
        cur.trk = c.wtrk
        scratch = [k.dram("xa", [S, D], F32), k.dram("xb", [S, D], F32)]
        setup_common(k, c)
        for pi, ph in enumerate(phases):
            dst = out if pi == len(phases) - 1 else scratch[pi % 2]
            if ph[0] == "f":
                l, i = int(ph[1]), int(ph[2])
                k.push_scope()
                alloc_ffn(k, c)
                ffn_phase(k, c, cur, dst, I["ffn_w_gate"][l, i], I["ffn_w_up"][l, i], I["ffn_w_down"][l, i],
                          I["ffn_norm"][l, i])
                k.pop_scope()
            elif ph == "attn":
                attn_layer(k, c, cur, dst, I)
            elif ph == "mix0":
                mix0_layer(k, c, cur, dst, I)
            elif ph == "m0p2":
                mix0_pass2(k, c, cur, dst, I, None, use_ya=False)
            elif ph.startswith("m0sim"):
                dbg = Buf(nc.dram_tensor("dbg", [4, 128, S], F32, kind="ExternalOutput").ap(), "dbg")
                yaT_d = k.dram("yaT_d", [4, 128, S], BF16)
                mix0_pass1(k, c, cur, I, yaT_d, ntiles=1, dbg=dbg, cut=ph[5:])
            elif ph == "m0dbg":
                dbg = Buf(nc.dram_tensor("dbg", [4, 128, S], F32, kind="ExternalOutput").ap(), "dbg")
                yaT_d = k.dram("yaT_d", [4, 128, S], BF16)
                mix0_pass1(k, c, cur, I, yaT_d, dbg=dbg)
                mix0_pass2(k, c, cur, dst, I, yaT_d)
            cur = dst
        k.barrier()
        print("instructions:", k.ninst, "sems:", k.nsem)
    return nc


_NC_CACHE = {}


def kernel(_phases=ALL_PHASES, **inputs):
    key = tuple(_phases)
    if key not in _NC_CACHE:
        _NC_CACHE[key] = build_program(_phases)
    nc = _NC_CACHE[key]
    cos, sin = rope_tables_np()
    in_maps = []
    for b in range(8):
        m = {}
        for n in EXT_SHAPES:
            if n == "rope_cos":
                m[n] = cos
            elif n == "rope_sin":
                m[n] = sin
            elif n == "x":
                m[n] = np.ascontiguousarray(np.asarray(inputs[n], dtype=np.float32)[b])
            else:
                m[n] = np.ascontiguousarray(np.asarray(inputs[n], dtype=np.float32))
        in_maps.append(m)
    res = run_bass_kernel_spmd(nc, in_maps, core_ids=list(range(8)))
    _NC_CACHE["last"] = res
    return np.stack([np.asarray(res.results[b]["out"]) for b in range(8)], axis=0)
```

```python
import math
from contextlib import ExitStack
import numpy as np
import concourse.bass as bass
import concourse.mybir as mybir
from concourse.bass_utils import run_bass_kernel_spmd

F32 = mybir.dt.float32
BF16 = mybir.dt.bfloat16
AF = mybir.ActivationFunctionType
ALU = mybir.AluOpType
AX = mybir.AxisListType

S = 4096
D = 1024
DFF = 2816
NFC = DFF // 128
SAME_ENG_SYNC = True
SEM_LIMIT = 30000


class Trk:
    __slots__ = ("lw", "rd", "dsem", "dcnt", "name", "psum")

    def __init__(self, name=""):
        self.psum = False
        self.lw = None
        self.rd = {}
        self.dsem = None
        self.dcnt = 0
        self.name = name


class V:
    __slots__ = ("ap", "trk")

    def __init__(self, ap, trk):
        self.ap = ap
        self.trk = trk


class Buf:
    def __init__(self, t, name=""):
        self.t = t
        self.trk = Trk(name)

    def __getitem__(self, idx):
        return V(self.t[idx], self.trk)

    def v(self, ap):
        return V(ap, self.trk)


class KB:
    def __init__(self, nc, es):
        self.nc = nc
        self.es = es
        self.E = {"pe": nc.tensor, "act": nc.scalar, "dve": nc.vector, "pool": nc.gpsimd, "sp": nc.sync}
        self.sem = {}
        self.cnt = {}
        self.seen = {e: {} for e in self.E}
        self.nsem = 0
        for e in ("pe", "act", "dve", "pool"):
            self.sem[e] = self.newsem(e)
            self.cnt[e] = 0
        self.ninst = 0
        self.scope = es
        self.dma_trks = []
        self.scope_trks = []
        self.free_dsems = []

    def push_scope(self):
        self.scope = ExitStack()
        self.scope_trks = []
        return self.scope

    def pop_scope(self):
        self.barrier()
        for t in self.scope_trks:
            if t.dsem is not None:
                self.free_dsems.append((t.dsem, t.dcnt))
                if t in self.dma_trks:
                    self.dma_trks.remove(t)
                t.dsem = None
        self.scope_trks = []
        self.scope.close()
        self.scope = self.es

    def barrier(self):
        toks = [(self.sem[e], self.cnt[e], e) for e in ("pe", "act", "dve", "pool") if self.cnt[e] > 0]
        toks += [(t.dsem, t.dcnt, "dma") for t in self.dma_trks]
        for e in self.E:
            self._wait(e, toks)

    def newsem(self, name):
        self.nsem += 1
        return self.es.enter_context(self.nc.semaphore(f"s{self.nsem}_{name}"))

    def uniq(self, name):
        self.nname = getattr(self, "nname", 0) + 1
        return f"{name}_{self.nname}"

    def sb(self, name, shape, dt):
        b = Buf(self.scope.enter_context(self.nc.sbuf_tensor(self.uniq(name), list(shape), dt)), name)
        if self.scope is not self.es:
            self.scope_trks.append(b.trk)
        return b

    def ps(self, name, shape=(128, 512), dt=F32):
        b = Buf(self.scope.enter_context(self.nc.psum_tensor(self.uniq(name), list(shape), dt)), name)
        b.trk.psum = True
        return b

    def dram(self, name, shape, dt, kind="Internal"):
        return Buf(self.nc.dram_tensor(name, list(shape), dt, kind=kind).ap(), name)

    def _deps(self, outs, ins, eng=None):
        deps = []
        for v in ins:
            if v.trk.lw is not None:
                deps.append(v.trk.lw)
            if v.trk.psum:
                deps.extend(t for e, t in v.trk.rd.items() if e != eng)
        for v in outs:
            if v.trk.lw is not None:
                deps.append(v.trk.lw)
            deps.extend(v.trk.rd.values())
        return deps

    def _wait(self, eng, deps):
        seen = self.seen[eng]
        best = {}
        for (sem, val, src) in deps:
            if src == eng and (eng == "pe" or not SAME_ENG_SYNC):
                continue
            key = id(sem)
            if seen.get(key, 0) >= val:
                continue
            if key not in best or best[key][1] < val:
                best[key] = (sem, val)
        for key, (sem, val) in best.items():
            self.E[eng].wait_ge(sem, val)
            seen[key] = val

    def op(self, eng, fn, outs, ins):
        outs = [o for o in outs if o is not None]
        ins = [i for i in ins if isinstance(i, V)]
        self._wait(eng, self._deps(outs, ins, eng))
        if self.cnt[eng] >= SEM_LIMIT:
            self.sem[eng] = self.newsem(eng)
            self.cnt[eng] = 0
        inst = fn()
        self.cnt[eng] += 1
        self.ninst += 1
        inst.then_inc(self.sem[eng], 1)
        tok = (self.sem[eng], self.cnt[eng], eng)
        for v in ins:
            v.trk.rd[eng] = tok
        for v in outs:
            v.trk.lw = tok
            v.trk.rd = {}
        return inst

    def dma(self, q, out, in_, **kw):
        self._wait(q, self._deps([out], [in_]))
        trk = out.trk
        if trk.dsem is None:
            if self.free_dsems:
                trk.dsem, trk.dcnt = self.free_dsems.pop()
            else:
                trk.dsem = self.newsem("d" + trk.name)
            self.dma_trks.append(trk)
        inst = self.E[q].dma_start(out=out.ap, in_=in_.ap, **kw)
        trk.dcnt += 16
        inst.then_inc(trk.dsem, 16)
        self.ninst += 1
        tok = (trk.dsem, trk.dcnt, "dma")
        in_.trk.rd["dma" + str(id(trk))] = tok
        out.trk.lw = tok
        out.trk.rd = {}
        return inst

    def mm(self, out, lhsT, rhs, start=True, stop=True, **kw):
        return self.op("pe", lambda: self.nc.tensor.matmul(out.ap, lhsT.ap, rhs.ap, start=start, stop=stop, **kw),
                       [out], [lhsT, rhs])

    def tr(self, out, in_, ident):
        return self.op("pe", lambda: self.nc.tensor.transpose(out.ap, in_.ap, ident.ap), [out], [in_, ident])

    def act(self, out, in_, func, bias=None, scale=None, accum_out=None, eng="act"):
        kw = {}
        if bias is not None:
            kw["bias"] = bias.ap if isinstance(bias, V) else bias
        if scale is not None:
            kw["scale"] = scale.ap if isinstance(scale, V) else scale
        if accum_out is not None:
            kw["accum_out"] = accum_out.ap
        return self.op("act", lambda: self.nc.scalar.activation(out.ap, in_.ap, func, **kw),
                       [out, accum_out], [in_, bias, scale])

    def ts(self, eng, out, in0, s1, s2, op0, op1=None, accum_out=None):
        e = self.E[eng]
        a1 = s1.ap if isinstance(s1, V) else s1
        a2 = s2.ap if isinstance(s2, V) else s2
        kw = {}
        if op1 is not None:
            kw["op1"] = op1
        if accum_out is not None:
            kw["accum_out"] = accum_out.ap
        return self.op(eng, lambda: e.tensor_scalar(out.ap, in0.ap, a1, a2, op0, **kw),
                       [out, accum_out], [in0, s1, s2])

    def tt(self, eng, out, in0, in1, op):
        e = self.E[eng]
        return self.op(eng, lambda: e.tensor_tensor(out.ap, in0.ap, in1.ap, op), [out], [in0, in1])

    def stt(self, out, in0, scalar, in1, op0, op1, eng="dve"):
        e = self.E[eng]
        sc = scalar.ap if isinstance(scalar, V) else scalar
        return self.op(eng, lambda: e.scalar_tensor_tensor(out.ap, in0.ap, sc, in1.ap, op0, op1),
                       [out], [in0, scalar, in1])

    def copy(self, eng, out, in_):
        e = self.E[eng]
        if eng == "act":
            return self.op(eng, lambda: e.copy(out.ap, in_.ap), [out], [in_])
        return self.op(eng, lambda: e.tensor_copy(out.ap, in_.ap), [out], [in_])

    def memset(self, eng, out, val):
        e = self.E[eng]
        return self.op(eng, lambda: e.memset(out.ap, val), [out], [])

    def recip(self, out, in_):
        return self.op("dve", lambda: self.nc.vector.reciprocal(out.ap, in_.ap), [out], [in_])

    def finish(self, trks):
        for t in trks:
            if t.dsem is not None:
                self.E["sp"].wait_ge(t.dsem, t.dcnt)


class Ctx:
    pass


def setup_common(k, c):
    nc = k.nc
    c.ident = k.sb("ident", [128, 128], F32)
    k.memset("pool", c.ident[:], 0.0)
    k.op("pool", lambda: nc.gpsimd.affine_select(c.ident.t[:], c.ident.t[:], pattern=[[-1, 128]],
                                                  compare_op=ALU.not_equal, fill=1.0, base=0,
                                                  channel_multiplier=1),
         [c.ident[:]], [c.ident[:]])
    c.mhalf = k.sb("mhalf", [128, 32], F32)
    k.memset("pool", c.mhalf[:, :], -0.5)


def alloc_ffn(k, c):
    c.wg = k.sb("wg", [128, 8, DFF], BF16)
    c.wu = k.sb("wu", [128, 8, DFF], BF16)
    c.wd = k.sb("wd", [128, NFC, D], BF16)
    c.xs = [k.sb(f"xs{i}", [128, D], F32) for i in range(2)]
    c.xr = [k.sb(f"xr{i}", [128, D], F32) for i in range(2)]
    c.xn = [k.sb(f"xn{i}", [128, D], F32) for i in range(2)]
    c.hT = [k.sb(f"hT{i}", [128, 8, 512], BF16) for i in range(2)]
    c.gT = k.sb("gT", [128, NFC, 512], BF16)
    c.sg = [k.sb(f"sg{i}", [128, 512], F32) for i in range(2)]
    c.ss = [k.sb(f"ss{i}", [128, 1], F32) for i in range(2)]
    c.gcol = k.sb("gcol", [128, 8], F32)
    c.ps = [k.ps(f"ps{i}") for i in range(8)]


def ffn_phase(k, c, xin, xout, wgate, wup, wdown, gnorm, first=False):
    nc = k.nc
    wgv = wgate.rearrange("(c p) f -> p c f", p=128)
    wuv = wup.rearrange("(c p) f -> p c f", p=128)
    wdv = wdown.rearrange("(c p) f -> p c f", p=128)
    H = DFF // 2
    for ci in range(8):
        for hh in range(2):
            k.dma("pool", c.wg[:, ci, hh * H:(hh + 1) * H], V(wgv[:, ci, hh * H:(hh + 1) * H], c.wtrk))
            k.dma("pool", c.wu[:, ci, hh * H:(hh + 1) * H], V(wuv[:, ci, hh * H:(hh + 1) * H], c.wtrk))
    for fi in range(NFC):
        k.dma("pool", c.wd[:, fi, :], V(wdv[:, fi, :], c.wtrk))
    k.dma("sp", c.gcol[:, :], V(gnorm.rearrange("(c p) -> p c", p=128), c.wtrk), allow_slow_non_contiguous=True)

    NT = S // 512
    cnt = [0]

    def norm_a(ti, s):
        i = cnt[0] % 2
        xs, xn, ss = c.xs[i], c.xn[i], c.ss[i]
        r0 = ti * 512 + s * 128
        k.dma("sp", xs[:, :], xin[r0:r0 + 128, :])
        rmsnorm_tile(k, c, xs[:, :], xn[:, :], ss[:, :], D, 1e-6)
        return i

    def norm_b(ti, s, i):
        xn = c.xn[i]
        hT = c.hT[ti % 2]
        cnt[0] += 1
        for half in range(2):
            pb = c.ps[half]
            for j in range(4):
                ci = half * 4 + j
                k.tr(pb[:, j * 128:(j + 1) * 128], xn[:, ci * 128:(ci + 1) * 128], c.ident[:, :])
            for j in range(4):
                ci = half * 4 + j
                if j % 2 == 0:
                    k.ts("dve", hT[:, ci, s * 128:(s + 1) * 128], pb[:, j * 128:(j + 1) * 128],
                         c.gcol[:, ci:ci + 1], None, ALU.mult)
                else:
                    k.act(hT[:, ci, s * 128:(s + 1) * 128], pb[:, j * 128:(j + 1) * 128],
                          AF.Copy, scale=c.gcol[:, ci:ci + 1])

    def gateup(ti):
        hT = c.hT[ti % 2]
        pend = {}
        for f in range(NFC):
            pg = c.ps[2 + (f % 2)]
            pu = c.ps[4 + (f % 2)]
            for ci in range(8):
                k.mm(pg[:, :], c.wg[:, ci, f * 128:(f + 1) * 128], hT[:, ci, :], start=(ci == 0), stop=(ci == 7))
            for ci in range(8):
                k.mm(pu[:, :], c.wu[:, ci, f * 128:(f + 1) * 128], hT[:, ci, :], start=(ci == 0), stop=(ci == 7))
            sg = c.sg[f % 2]
            k.act(sg[:, :], pg[:, :], AF.Silu)
            k.tt("dve", c.gT[:, f, :], sg[:, :], pu[:, :], ALU.mult)
            if ti + 1 < NT:
                if f in (1, 6, 11, 16):
                    s = (1, 6, 11, 16).index(f)
                    pend[s] = norm_a(ti + 1, s)
                if f in (5, 10, 15, 20):
                    s = (5, 10, 15, 20).index(f)
                    norm_b(ti + 1, s, pend[s])

    def down(ti):
        for s in range(4):
            r0 = ti * 512 + s * 128
            xr = c.xr[s % 2]
            k.dma("sp", xr[:, :], xin[r0:r0 + 128, :])
            for dh in range(2):
                po = c.ps[6 + dh]
                for f in range(NFC):
                    k.mm(po[:, :], c.gT[:, f, s * 128:(s + 1) * 128], c.wd[:, f, dh * 512:(dh + 1) * 512],
                         start=(f == 0), stop=(f == NFC - 1))
                k.stt(xr[:, dh * 512:(dh + 1) * 512], po[:, :], 0.5, xr[:, dh * 512:(dh + 1) * 512],
                      ALU.mult, ALU.add)
            k.dma("pool", xout[r0:r0 + 128, :], xr[:, :])

    for s in range(4):
        i = norm_a(0, s)
        norm_b(0, s, i)
    for ti in range(NT):
        gateup(ti)
        down(ti)


LAM_INIT1 = 0.8 - 0.6 * math.exp(-0.3 * 1)
ATT_SCALE = 64 ** -0.5


def rmsnorm_tile(k, c, xs, xn, ss, dim, eps):
    k.act(xn, xs, AF.Square, accum_out=ss)
    k.ts("dve", ss, ss, 1.0 / dim, eps, ALU.mult, ALU.add)
    k.tt("pool", ss, ss, c.mhalf[:, 0:1], ALU.pow)
    k.ts("dve", xn, xs, ss, None, ALU.mult)


def attn_layer(k, c, xin, xout, I):
    nc = k.nc
    sc = k.push_scope()
    W = c.wtrk
    NTT = S // 128
    qkT_d = k.dram("qkT_d", [16, 128, S], BF16)
    v_d = k.dram("v_d", [S, D], BF16)
    oT_d = k.dram("oT_d", [8, 128, S], BF16)
    ident = c.ident
    identb = k.sb("identb", [128, 128], BF16)
    k.copy("dve", identb[:, :], ident[:, :])
    win = k.sb("a_win", [128, 8, 3 * D], BF16)
    wout = k.sb("a_wout", [128, 8, D], BF16)
    winv = I["c_w_in"][0].rearrange("(c p) f -> p c f", p=128)
    woutv = I["c_w_out"][0].rearrange("(c p) f -> p c f", p=128)
    for ci in range(8):
        for hh in range(2):
            k.dma("pool", win[:, ci, hh * 1536:(hh + 1) * 1536], V(winv[:, ci, hh * 1536:(hh + 1) * 1536], W))
    for ci in range(8):
        k.dma("pool", wout[:, ci, :], V(woutv[:, ci, :], W))
    gcol = k.sb("a_gcol", [128, 8], F32)
    k.dma("sp", gcol[:, :], V(I["mix_norm"][1].rearrange("(c p) -> p c", p=128), W), allow_slow_non_contiguous=True)
    def bc_load(name, ap, n):
        t = k.sb(name, [128, n], F32)
        k.dma("sp", t[:, :], V(ap.partition_broadcast(128), W))
        return t
    qg = bc_load("a_qg", I["c_q_norm"][0], 64)
    kg = bc_load("a_kg", I["c_k_norm"][0], 64)
    lq1 = bc_load("a_lq1", I["c_lq1"][0], 64)
    lk1 = bc_load("a_lk1", I["c_lk1"][0], 64)
    lq2 = bc_load("a_lq2", I["c_lq2"][0], 64)
    lk2 = bc_load("a_lk2", I["c_lk2"][0], 64)
    gsub = bc_load("a_gsub", I["c_sub_norm"][0], 128)
    k.ts("dve", gsub[:, :], gsub[:, :], 1.0 - LAM_INIT1, None, ALU.mult)
    e1 = k.sb("a_e1", [128, 1], F32)
    e2 = k.sb("a_e2", [128, 1], F32)
    nlam = k.sb("a_nlam", [128, 1], F32)
    k.tt("dve", lq1[:, :], lq1[:, :], lk1[:, :], ALU.mult)
    k.op("dve", lambda: nc.vector.tensor_reduce(out=e1.t[:, :], in_=lq1.t[:, :], axis=AX.X, op=ALU.add), [e1[:, :]], [lq1[:, :]])
    k.tt("dve", lq2[:, :], lq2[:, :], lk2[:, :], ALU.mult)
    k.op("dve", lambda: nc.vector.tensor_reduce(out=e2.t[:, :], in_=lq2.t[:, :], axis=AX.X, op=ALU.add), [e2[:, :]], [lq2[:, :]])
    k.act(e1[:, :], e1[:, :], AF.Exp)
    k.act(e2[:, :], e2[:, :], AF.Exp)
    k.tt("dve", nlam[:, :], e2[:, :], e1[:, :], ALU.subtract)
    k.ts("dve", nlam[:, :], nlam[:, :], -LAM_INIT1, None, ALU.add)
    cosb = k.sb("a_cos", [128, NTT, 32], F32)
    sinb = k.sb("a_sin", [128, NTT, 32], F32)
    k.dma("sp", cosb[:, :, :], V(I["rope_cos"].rearrange("(t p) j -> p t j", p=128), W))
    k.dma("sp", sinb[:, :, :], V(I["rope_sin"].rearrange("(t p) j -> p t j", p=128), W))
    trif = k.sb("a_trif", [128, 128], F32)
    tri = k.sb("a_tri", [128, 128], BF16)
    k.memset("pool", trif[:, :], 1.0)
    k.op("pool", lambda: nc.gpsimd.affine_select(trif.t[:, :], trif.t[:, :], pattern=[[1, 128]],
                                                  compare_op=ALU.is_ge, fill=0.0, base=0, channel_multiplier=-1),
         [trif[:, :]], [trif[:, :]])
    k.copy("pool", tri[:, :], trif[:, :])

    ps_t = [k.ps(f"a_pst{i}") for i in range(2)]
    ps_z = [k.ps(f"a_psz{i}") for i in range(4)]
    ps_b = [k.ps(f"a_psb{i}", (128, 1024), BF16) for i in range(2)]
    xs2 = [k.sb(f"a_xs{i}", [128, D], F32) for i in range(2)]
    xn = k.sb("a_xn", [128, D], F32)
    ssn = k.sb("a_ssn", [128, 1], F32)
    hTt = [k.sb(f"a_hT{i}", [128, 8, 128], BF16) for i in range(2)]
    sq = k.sb("a_sq", [128, 512], F32)
    ssg = k.sb("a_ssg", [128, 32], F32)
    zn = [k.sb(f"a_zn{i}", [128, 8, 64], F32) for i in range(2)]
    tmpa = [k.sb(f"a_ta{i}", [128, 8, 32], F32) for i in range(2)]
    tmpb = [k.sb(f"a_tb{i}", [128, 8, 32], F32) for i in range(2)]
    cg = k.sb("a_cg", [128, 2, 2, 32], F32)
    sg_ = k.sb("a_sg", [128, 2, 2, 32], F32)
    qkr = [k.sb(f"a_qkr{i}", [128, 16, 128], BF16) for i in range(2)]
    qkT = [k.sb(f"a_qkT{i}", [128, 16, 128], BF16) for i in range(2)]
    vt = [k.sb(f"a_vt{i}", [128, D], BF16) for i in range(2)]
    qkT_dv = qkT_d.t.rearrange("j p s -> p j s")

    def a1_part1(t):
        xs = xs2[t % 2]
        hT = hTt[t % 2]
        k.dma("sp", xs[:, :], xin[t * 128:(t + 1) * 128, :])
        rmsnorm_tile(k, c, xs[:, :], xn[:, :], ssn[:, :], D, 1e-6)
        for half in range(2):
            pb = ps_t[half]
            for j in range(4):
                ci = half * 4 + j
                k.tr(pb[:, j * 128:(j + 1) * 128], xn[:, ci * 128:(ci + 1) * 128], ident[:, :])
            for j in range(4):
                ci = half * 4 + j
                if j % 2 == 0:
                    k.ts("dve", hT[:, ci, :], pb[:, j * 128:(j + 1) * 128], gcol[:, ci:ci + 1], None, ALU.mult)
                else:
                    k.act(hT[:, ci, :], pb[:, j * 128:(j + 1) * 128], AF.Copy, scale=gcol[:, ci:ci + 1])
        for w_, g_ in ((0, qg), (1, kg)):
            k.tt("pool", cg[:, w_, 0, :], cosb[:, t, :], g_[:, 0:32], ALU.mult)
            k.tt("pool", cg[:, w_, 1, :], cosb[:, t, :], g_[:, 32:64], ALU.mult)
            k.tt("pool", sg_[:, w_, 0, :], sinb[:, t, :], g_[:, 32:64], ALU.mult)
            k.tt("pool", sg_[:, w_, 1, :], sinb[:, t, :], g_[:, 0:32], ALU.mult)
        qr = qkr[t % 2]
        pzs = []
        for nb in range(6):
            pz = ps_z[nb % 4]
            for ci in range(8):
                k.mm(pz[:, :], hT[:, ci, :], win[:, ci, nb * 512:(nb + 1) * 512], start=(ci == 0), stop=(ci == 7))
            if nb < 4:
                k.act(sq[:, :], pz[:, :], AF.Square)
                k.op("dve", lambda: nc.vector.tensor_reduce(out=ssg.t[:, nb * 8:(nb + 1) * 8],
                                                             in_=sq.t[:, :].rearrange("p (g d) -> p g d", d=64),
                                                             axis=AX.X, op=ALU.add),
                     [ssg[:, nb * 8:(nb + 1) * 8]], [sq[:, :]])
                rs = ssg[:, nb * 8:(nb + 1) * 8]
                k.ts("dve", rs, rs, 1.0 / 64, 1e-6, ALU.mult, ALU.add)
                k.tt("pool", rs, rs, c.mhalf[:, 0:8], ALU.pow)
                z_ = zn[nb % 2]
                k.tt("dve", z_[:, :, :], V(pz.t[:, :].rearrange("p (g d) -> p g d", d=64), pz.trk),
                     V(ssg.t[:, nb * 8:(nb + 1) * 8].unsqueeze(2).to_broadcast([128, 8, 64]), ssg.trk), ALU.mult)
                w_ = nb // 2
                ta = tmpa[nb % 2]
                tb = tmpb[nb % 2]
                def bc(tile_, i0, i1):
                    return V(tile_.t[:, i0, i1, :].unsqueeze(1).to_broadcast([128, 8, 32]), tile_.trk)
                o1 = qr[:, (nb * 4):(nb * 4 + 4), :]
                ov = qr.t[:, nb * 4:(nb + 1) * 4, :].rearrange("p h (c d) -> p (h c) d", d=64)
                k.tt("dve", ta[:, :, :], z_[:, :, 0:32], bc(cg, w_, 0), ALU.mult)
                k.tt("pool", tb[:, :, :], z_[:, :, 32:64], bc(sg_, w_, 0), ALU.mult)
                k.tt("dve", V(ov[:, :, 0:32], qr.trk), ta[:, :, :], tb[:, :, :], ALU.subtract)
                ta2 = tmpa[(nb + 1) % 2]
                tb2 = tmpb[(nb + 1) % 2]
                k.tt("pool", ta2[:, :, :], z_[:, :, 32:64], bc(cg, w_, 1), ALU.mult)
                k.tt("dve", tb2[:, :, :], z_[:, :, 0:32], bc(sg_, w_, 1), ALU.mult)
                k.tt("pool", V(ov[:, :, 32:64], qr.trk), ta2[:, :, :], tb2[:, :, :], ALU.add)
            else:
                k.act(vt[t % 2][:, (nb - 4) * 512:(nb - 3) * 512], pz[:, :], AF.Copy)
        k.dma("act", v_d[t * 128:(t + 1) * 128, :], vt[t % 2][:, :])

    def a1_part2(t):
        qr = qkr[t % 2]
        qT = qkT[t % 2]
        for half in range(2):
            pb = ps_b[half]
            for j in range(8):
                k.tr(pb[:, j * 128:(j + 1) * 128], qr[:, half * 8 + j, :], identb[:, :])
            if half == 0:
                k.copy("dve", qT[:, 0:8, :], V(pb.t[:, :].rearrange("p (j s) -> p j s", s=128), pb.trk))
            else:
                k.copy("act", qT[:, 8:16, :], V(pb.t[:, :].rearrange("p (j s) -> p j s", s=128), pb.trk))
        k.dma("act", V(qkT_dv[:, :, t * 128:(t + 1) * 128], qkT_d.trk), qT[:, :, :])

    a1_part1(0)
    for t in range(NTT):
        if t + 1 < NTT:
            a1_part1(t + 1)
        a1_part2(t)

    ps_s = [ps_z[0], ps_z[1]]
    ps_acc = [[ps_t[0], ps_t[1]], [ps_z[2], ps_z[3]]]
    ps_o = ps_b[0]
    qTh = [k.sb(f"a_qTh{i}", [128, S], BF16) for i in range(2)]
    kTh = [[k.sb(f"a_kTh{i}_{cc}", [128, S], BF16) for cc in range(2)] for i in range(2)]
    for i in range(2):
        k.memset("pool", kTh[i][0][64:128, :], 0.0)
        k.memset("pool", kTh[i][1][0:64, :], 0.0)
    Vh = [k.sb(f"a_Vh{i}", [128, NTT, 132], BF16) for i in range(2)]
    for i in range(2):
        k.memset("pool", Vh[i][:, :, 128:129], 1.0)
    NPT = 4
    pT = [k.sb(f"a_pT{i}", [128, 512], BF16) for i in range(NPT)]
    o0 = [k.sb(f"a_o0{i}", [128, 4, 128], F32) for i in range(2)]
    oc = [k.sb(f"a_oc{i}", [128, 128], F32) for i in range(4)]
    osq = [k.sb(f"a_osq{i}", [128, 128], F32) for i in range(2)]
    NSLOT = 4
    onb = [k.sb(f"a_onb{i}", [128, 4, 128], BF16) for i in range(NSLOT)]
    rl = [k.sb(f"a_rl{i}", [128, 1], F32) for i in range(4)]
    sso = [k.sb(f"a_sso{i}", [128, 1], F32) for i in range(2)]
    oTs = [k.sb(f"a_oTs{i}", [128, 512], BF16) for i in range(2)]
    v_dv = v_d.t.rearrange("(kt p) f -> p kt f", p=128)

    def load_head(h):
        b = h % 2
        k.dma("sp", qTh[b][:, :], qkT_d[h, :, :])
        k.dma("sp", kTh[b][0][0:64, :], qkT_d[8 + h, 0:64, :])
        k.dma("sp", kTh[b][1][64:128, :], qkT_d[8 + h, 64:128, :])
        k.dma("sp", Vh[b][:, :, 0:128], V(v_dv[:, :, h * 128:(h + 1) * 128], v_d.trk))

    groups = [(h, qt, cc) for h in range(8) for qt in range(8) for cc in range(2)]
    pairs = []
    for gi, (h, qt, cc) in enumerate(groups):
        nk = 4 * qt + 4
        for kt in range(nk):
            pairs.append((gi, kt, kt == nk - 1))

    def emit_score(pi):
        gi, kt, _ = pairs[pi]
        h, qt, cc = groups[gi]
        b = h % 2
        r = kt * 128 - qt * 512
        j0 = max(r, 0)
        pss = ps_s[pi % 2]
        pt = pT[pi % NPT]
        k.mm(pss[:, j0:512], kTh[b][cc][:, kt * 128:(kt + 1) * 128],
             qTh[b][:, qt * 512 + j0:(qt + 1) * 512])
        k.act(pt[:, j0:512], pss[:, j0:512], AF.Exp, scale=ATT_SCALE)
        if r >= 0:
            k.tt("pool", pt[:, r:r + 128], pt[:, r:r + 128], tri[:, :], ALU.mult)

    def emit_pv(pi):
        gi, kt, _ = pairs[pi]
        h, qt, cc = groups[gi]
        b = h % 2
        r = kt * 128 - qt * 512
        acc = ps_acc[gi % 2]
        pt = pT[pi % NPT]
        for qs in range(4):
            if qs * 128 + 127 < r:
                continue
            last_kt = qt * 4 + qs
            k.mm(acc[qs // 2][:, (qs % 2) * 132:(qs % 2) * 132 + 129],
                 pt[:, qs * 128:(qs + 1) * 128], Vh[b][:, kt, 0:129],
                 start=(kt == 0 and qs % 2 == 0), stop=(kt == last_kt), skip_group_check=True)

    def emit_epi_a(gi, slot):
        h, qt, cc = groups[gi]
        acc = ps_acc[gi % 2]
        for qs in range(4):
            a_ = acc[qs // 2]
            base = (qs % 2) * 132
            r_ = rl[qs]
            k.recip(r_[:, :], a_[:, base + 128:base + 129])
            if cc == 0:
                k.ts("dve", o0[qt % 2][:, qs, :], a_[:, base:base + 128], r_[:, 0:1], None, ALU.mult)
            else:
                k.tt("dve", r_[:, :], r_[:, :], nlam[:, :], ALU.mult)
                k.stt(oc[qs][:, :], a_[:, base:base + 128], r_[:, 0:1], o0[qt % 2][:, qs, :], ALU.mult, ALU.add)
        if cc == 0:
            return
        for qs in range(4):
            oc_ = oc[qs]
            ss_ = sso[qs % 2]
            k.tt("pool", osq[qs % 2][:, :], oc_[:, :], oc_[:, :], ALU.mult)
            k.op("dve", lambda: nc.vector.tensor_reduce(out=ss_.t[:, :], in_=osq[qs % 2].t[:, :], axis=AX.X, op=ALU.add),
                 [ss_[:, :]], [osq[qs % 2][:, :]])
            k.ts("dve", ss_[:, :], ss_[:, :], 1.0 / 128, 1e-5, ALU.mult, ALU.add)
            k.tt("pool", ss_[:, :], ss_[:, :], c.mhalf[:, 0:1], ALU.pow)
            k.stt(onb[slot][:, qs, :], oc_[:, :], ss_[:, 0:1], gsub[:, :], ALU.mult, ALU.mult)

    def emit_epi_b(gi, slot):
        h, qt, cc = groups[gi]
        for qs in range(4):
            k.tr(ps_o[:, qs * 128:(qs + 1) * 128], onb[slot][:, qs, :], identb[:, :])
        ot = oTs[qt % 2]
        k.copy("dve", ot[:, :], ps_o[:, 0:512])
        k.dma("sp", oT_d[h, :, qt * 512:(qt + 1) * 512], ot[:, :])

    LA = 2
    DEFER = 16
    deferred = []
    nslot = 0
    load_head(0)
    cur_h = -1
    npairs = len(pairs)
    for i in range(npairs + LA):
        if i < npairs:
            emit_score(i)
        j = i - LA
        if j >= 0:
            h = groups[pairs[j][0]][0]
            if h != cur_h:
                cur_h = h
                if h + 1 < 8:
                    load_head(h + 1)
            emit_pv(j)
            gi, kt, is_last = pairs[j]
            if is_last:
                if groups[gi][2] == 1:
                    slot = nslot % NSLOT
                    nslot += 1
                    for d_ in [d_ for d_ in deferred if d_[2] == slot]:
                        deferred.remove(d_)
                        emit_epi_b(d_[1], d_[2])
                    emit_epi_a(gi, slot)
                    deferred.append((j + DEFER, gi, slot))
                else:
                    emit_epi_a(gi, None)
            while deferred and deferred[0][0] <= j:
                d_ = deferred.pop(0)
                emit_epi_b(d_[1], d_[2])
    while deferred:
        d_ = deferred.pop(0)
        emit_epi_b(d_[1], d_[2])

    oTt = qkT
    oT_dv = oT_d.t.rearrange("h p s -> p h s")
    for t in range(NTT):
        xs = xs2[t % 2]
        ot = oTt[t % 2]
        k.dma("sp", xs[:, :], xin[t * 128:(t + 1) * 128, :])
        k.dma("sp", ot[:, 0:8, :], V(oT_dv[:, :, t * 128:(t + 1) * 128], oT_d.trk))
        for dh in range(2):
            po = ps_z[dh]
            for hh in range(8):
                k.mm(po[:, :], ot[:, hh, :], wout[:, hh, dh * 512:(dh + 1) * 512], start=(hh == 0), stop=(hh == 7))
            k.tt("dve", xs[:, dh * 512:(dh + 1) * 512], po[:, :], xs[:, dh * 512:(dh + 1) * 512], ALU.add)
        k.dma("pool", xout[t * 128:(t + 1) * 128, :], xs[:, :])
    k.pop_scope()

GN_EPS_ = 64e-5
NEG_E05 = -math.exp(-0.5)


def col_load(k, c, name, ap1d, ncol):
    t = k.sb(name, [128, ncol], F32)
    k.dma("sp", t[:, :], V(ap1d.rearrange("(c p) -> p c", p=128), c.wtrk), allow_slow_non_contiguous=True)
    return t


def norm_tile_to_hT(k, c, xs, xn, ss, ps_pair, hT, s, gcol):
    rmsnorm_tile(k, c, xs[:, :], xn[:, :], ss[:, :], D, 1e-6)
    for half in range(2):
        pb = ps_pair[half]
        for j in range(4):
            ci = half * 4 + j
            k.tr(pb[:, j * 128:(j + 1) * 128], xn[:, ci * 128:(ci + 1) * 128], c.ident[:, :])
        for j in range(4):
            ci = half * 4 + j
            if j % 2 == 0:
                k.ts("dve", hT[:, ci, s * 128:(s + 1) * 128], pb[:, j * 128:(j + 1) * 128], gcol[:, ci:ci + 1], None, ALU.mult)
            else:
                k.act(hT[:, ci, s * 128:(s + 1) * 128], pb[:, j * 128:(j + 1) * 128], AF.Copy, scale=gcol[:, ci:ci + 1])


def mix0_pass2(k, c, xin, xout, I, yaT_d, use_ya=True):
    nc = k.nc
    k.push_scope()
    W = c.wtrk
    ident = c.ident
    P = [k.ps(f"m2_ps{i}") for i in range(8)]
    winu = k.sb("m2_winu", [128, 8, 1024], BF16)
    wout = k.sb("m2_wout", [128, 8, 1024], BF16)
    winv = I["a_w_in"][0].rearrange("(c p) f -> p c f", p=128)
    woutv = I["e_w_out"][0].rearrange("(c p) f -> p c f", p=128)
    for ci in range(8):
        k.dma("pool", winu[:, ci, :], V(winv[:, ci, 1792:2816], W))
    for ci in range(8):
        k.dma("pool", wout[:, ci, :], V(woutv[:, ci, :], W))
    gcol = col_load(k, c, "m2_gcol", I["mix_norm"][0], 8)
    gb = col_load(k, c, "m2_gb", I["b_glu_bias"][0], 8)
    dwb = col_load(k, c, "m2_dwb", I["b_dw_bias"][0], 4)
    clw = col_load(k, c, "m2_clw", I["b_ln_w"][0], 4)
    clb = col_load(k, c, "m2_clb", I["b_ln_b"][0], 4)
    dwr = k.sb("m2_dwr", [31, 512], F32)
    k.dma("sp", dwr[:, :], V(I["b_dw"][0], W))
    dwT = k.sb("m2_dwT", [128, 4, 32], F32)
    for i in range(4):
        k.tr(P[0][:, i * 32:i * 32 + 31], dwr[0:31, i * 128:(i + 1) * 128], ident[0:31, 0:31])
    k.copy("dve", dwT[:, :, :], V(P[0].t[:, 0:128].rearrange("p (i j) -> p i j", j=32), P[0].trk))
    Dg = k.sb("m2_Dg", [128, 4, 31, 128], BF16)
    n = 0
    for i in range(4):
        for j in range(31):
            k.ts("dve" if n % 2 == 0 else "pool", Dg[:, i, j, :], ident[:, :], dwT[:, i, j:j + 1], None, ALU.mult)
            n += 1
    ones512 = k.sb("m2_ones", [128, 128], F32)
    k.memset("pool", ones512[:, :], 1.0 / 512)
    eps5 = k.sb("m2_eps", [128, 1], F32)
    k.memset("pool", eps5[:, :], 1e-5)
    glT = [k.sb(f"m2_glT{p}", [128, 4, 542], BF16) for p in range(2)]
    k.memset("pool", glT[0][:, :, 0:30], 0.0)
    xs4 = [[k.sb(f"m2_xs{p}_{i}", [128, D], F32) for i in range(4)] for p in range(2)]
    xn = k.sb("m2_xn", [128, D], F32)
    ssn = k.sb("m2_ssn", [128, 1], F32)
    hT = [k.sb(f"m2_hT{p}", [128, 8, 512], BF16) for p in range(2)]
    sig = [k.sb(f"m2_sig{i}", [128, 512], F32) for i in range(2)]
    cb = [k.sb(f"m2_cb{i}", [128, 512], F32) for i in range(4)]
    dd = [k.sb(f"m2_dd{i}", [128, 512], F32) for i in range(4)]
    dsq = [k.sb(f"m2_dsq{i}", [128, 512], F32) for i in range(2)]
    rstd = k.sb("m2_rstd", [128, 512], F32)
    yT = [k.sb(f"m2_yT{p}", [128, 8, 512], BF16) for p in range(2)]
    if not use_ya:
        for p in range(2):
            k.memset("pool", yT[p][:, 0:4, :], 0.0)
    yaT_dv = yaT_d.t.rearrange("h p s -> p h s") if use_ya else None
    NT2 = S // 512

    def S1(ti):
        p = ti % 2
        for s in range(4):
            r0 = ti * 512 + s * 128
            k.dma("sp", xs4[p][s][:, :], xin[r0:r0 + 128, :])
            norm_tile_to_hT(k, c, xs4[p][s], xn, ssn, (P[0], P[1]), hT[p], s, gcol)
        if use_ya:
            k.dma("sp", yT[p][:, 0:4, :], V(yaT_dv[:, :, ti * 512:(ti + 1) * 512], yaT_d.trk))
        if ti > 0:
            for i in range(4):
                k.copy("pool", glT[p][:, i, 0:30], glT[1 - p][:, i, 512:542])
        for i in range(4):
            pa, pb_ = P[2], P[3]
            for ci in range(8):
                k.mm(pa[:, :], winu[:, ci, i * 128:(i + 1) * 128], hT[p][:, ci, :], start=(ci == 0), stop=(ci == 7))
            for ci in range(8):
                k.mm(pb_[:, :], winu[:, ci, (4 + i) * 128:(5 + i) * 128], hT[p][:, ci, :], start=(ci == 0), stop=(ci == 7))
            sg = sig[i % 2]
            k.act(sg[:, :], pb_[:, :], AF.Sigmoid, bias=gb[:, 4 + i:5 + i])
            k.stt(glT[p][:, i, 30:542], pa[:, :], gb[:, i:i + 1], sg[:, :], ALU.add, ALU.mult)

    def S2(ti):
        p = ti % 2
        for i in range(4):
            pc = P[4 + (i % 2)]
            for j in range(31):
                k.mm(pc[:, :], Dg[:, i, j, :], glT[p][:, i, j:j + 512], start=(j == 0), stop=(j == 30))
            k.ts("dve", cb[i][:, :], pc[:, :], dwb[:, i:i + 1], None, ALU.add)
        pm = P[6]
        for i in range(4):
            k.mm(pm[:, :], ones512[:, :], cb[i][:, :], start=(i == 0), stop=(i == 3))
        for i in range(4):
            k.tt("dve", dd[i][:, :], cb[i][:, :], pm[:, :], ALU.subtract)
        pv = P[7]
        for i in range(4):
            q_ = dsq[i % 2]
            k.act(q_[:, :], dd[i][:, :], AF.Square)
            k.mm(pv[:, :], ones512[:, :], q_[:, :], start=(i == 0), stop=(i == 3))
        k.act(rstd[:, :], pv[:, :], AF.Sqrt, bias=eps5[:, 0:1])
        k.recip(rstd[:, :], rstd[:, :])
        for i in range(4):
            k.tt("dve" if i % 2 == 0 else "pool", dd[i][:, :], dd[i][:, :], rstd[:, :], ALU.mult)
            k.act(yT[p][:, 4 + i, :], dd[i][:, :], AF.Silu, scale=clw[:, i:i + 1], bias=clb[:, i:i + 1])
        for s in range(4):
            xs = xs4[p][s]
            for dh in range(2):
                po = P[4 + dh]
                for fc in range(8):
                    k.mm(po[:, :], yT[p][:, fc, s * 128:(s + 1) * 128], wout[:, fc, dh * 512:(dh + 1) * 512],
                         start=(fc == 0), stop=(fc == 7))
                k.tt("dve", xs[:, dh * 512:(dh + 1) * 512], po[:, :], xs[:, dh * 512:(dh + 1) * 512], ALU.add)
            r0 = ti * 512 + s * 128
            k.dma("pool", xout[r0:r0 + 128, :], xs[:, :])

    S1(0)
    for ti in range(NT2):
        if ti + 1 < NT2:
            S1(ti + 1)
        S2(ti)
    k.pop_scope()


class _Cut(Exception):
    pass


def mix0_pass1(k, c, xin, I, yaT_d, ntiles=S // 512, dbg=None, cut=None):
    nc = k.nc
    k.push_scope()
    W = c.wtrk
    ident = c.ident
    P = [k.ps(f"m1_ps{i}") for i in range(7)]
    PB = k.ps("m1_psb", (128, 1024), BF16)
    identb = k.sb("m1_identb", [128, 128], BF16)
    k.copy("dve", identb[:, :], ident[:, :])
    win = k.sb("m1_win", [128, 8, 1792], BF16)
    winv = I["a_w_in"][0].rearrange("(c p) f -> p c f", p=128)
    for ci in range(8):
        k.dma("pool", win[:, ci, :], V(winv[:, ci, 0:1792], W))
    w2a2 = k.sb("m1_w2a2", [128, 512], BF16)
    k.dma("pool", w2a2[0:64, :], V(I["a_w2"][0], W))
    k.dma("pool", w2a2[64:128, :], V(I["a_a2"][0], W))
    g2b = k.sb("m1_g2b", [128, 512], BF16)
    k.dma("pool", g2b[:, :], V(I["a_g2"][0], W))
    gcol = col_load(k, c, "m1_gcol", I["mix_norm"][0], 8)
    mu = col_load(k, c, "m1_mu", I["a_mu"][0], 14)
    omu = k.sb("m1_omu", [128, 14], F32)
    k.ts("dve", omu[:, :], mu[:, :], -1.0, 1.0, ALU.mult, ALU.add)
    w0c = col_load(k, c, "m1_w0", I["a_w0"][0], 4)
    a0c = col_load(k, c, "m1_a0", I["a_a0"][0], 4)
    kkc = col_load(k, c, "m1_kk", I["a_k_k"][0], 4)
    kac = col_load(k, c, "m1_ka", I["a_k_a"][0], 4)
    okac = k.sb("m1_oka", [128, 4], F32)
    k.ts("dve", okac[:, :], kac[:, :], -1.0, 1.0, ALU.mult, ALU.add)
    rkc = col_load(k, c, "m1_rk", I["a_r_k"][0].rearrange("h d -> (h d)"), 4)
    lnw = col_load(k, c, "m1_lnw", I["a_ln_w"][0], 4)
    lnb = col_load(k, c, "m1_lnb", I["a_ln_b"][0], 4)
    bones = k.sb("m1_bones", [128, 128], F32)
    k.memset("pool", bones[:, :], 0.0)
    k.memset("pool", bones[0:64, 0:64], 1.0)
    k.memset("pool", bones[64:128, 64:128], 1.0)
    bones64 = k.sb("m1_bones64", [128, 128], F32)
    k.ts("dve", bones64[:, :], bones[:, :], 1.0 / 64, None, ALU.mult)
    onesT = k.sb("m1_onesT", [128, 128], F32)
    k.memset("pool", onesT[:, :], 1.0)
    epsg = k.sb("m1_epsg", [128, 1], F32)
    k.memset("pool", epsg[:, :], GN_EPS_)
    MSK = k.sb("m1_MSK", [128, 512], F32)
    MSKL = k.sb("m1_MSKL", [128, 512], F32)
    k.memset("pool", MSK[:, :], 1.0)
    k.memset("pool", MSKL[:, :], 1.0)
    for q in range(4):
        strict = (q % 2 == 0)
        blk = MSK[:, q * 128:(q + 1) * 128]
        k.op("pool", lambda: nc.gpsimd.affine_select(blk.ap, blk.ap, pattern=[[1, 128]], compare_op=ALU.is_ge,
                                                      fill=0.0, base=(-1 if strict else 0), channel_multiplier=-1),
             [blk], [blk])
        blk2 = MSKL[:, q * 128:(q + 1) * 128]
        k.op("pool", lambda: nc.gpsimd.affine_select(blk2.ap, blk2.ap, pattern=[[-1, 128]], compare_op=ALU.is_ge,
                                                      fill=0.0, base=-1, channel_multiplier=1),
             [blk2], [blk2])
    zlast = k.sb("m1_zlast", [128, 14], F32)
    k.memset("pool", zlast[:, :], 0.0)
    S2 = [k.sb(f"m1_S2{i}", [128, 4, 128], F32) for i in range(2)]
    k.memset("pool", S2[0][:, :, :], 0.0)
    k.memset("pool", S2[1][:, :, :], 0.0)
    PhiT2 = k.sb("m1_PhiT2", [128, 4, 128], F32)
    k.memset("pool", PhiT2[:, :, :], 0.0)
    Psi = k.sb("m1_Psi", [128, 4, 128], F32)
    k.memset("pool", Psi[:, :, :], 0.0)
    xs2 = [k.sb(f"m1_xs{i}", [128, D], F32) for i in range(2)]
    xn = k.sb("m1_xn", [128, D], F32)
    ssn = k.sb("m1_ssn", [128, 1], F32)
    hT = k.sb("m1_hT", [128, 8, 512], BF16)
    zraw = [k.sb(f"m1_zraw{i}", [128, 513], F32) for i in range(2)]
    ztmp = [k.sb(f"m1_ztmp{i}", [128, 512], F32) for i in range(2)]
    zs12 = k.sb("m1_zs12", [128, 512], F32)
    zs13 = k.sb("m1_zs13", [128, 512], F32)
    c12b = k.sb("m1_c12b", [128, 512], BF16)
    sgd = k.sb("m1_sgd", [128, 512], BF16)
    def f32t(name):
        return k.sb("m1f_" + name, [128, 512], F32)
    r_t, k_t, v_t = f32t("r"), f32t("k"), f32t("v")
    lw, a_t, kk, kk2, kkn, ka, kmod, Bq, rkr, cl, Pinc, Pexc, Pinv, E2 = [f32t(n) for n in
        ["lw", "a", "kk", "kk2", "kkn", "ka", "kmod", "Bq", "rkr", "cl", "Pinc", "Pexc", "Pinv", "E2"]]
    gT = [f32t(f"gT{i}") for i in range(4)]
    bon = [f32t(f"bon{i}") for i in range(4)]
    YT = [f32t(f"YT{i}") for i in range(4)]
    PC = k.sb("m1_PC", [128, 4, 4], F32)
    ART = [k.sb(f"m1_ART{i}", [128, 2, 512], BF16) for i in range(4)]
    def b16t(name):
        return [k.sb(f"m1_{name}{i}", [128, 512], BF16) for i in range(4)]
    BT, KT, BPT, KPT, vTb = b16t("BT"), b16t("KT"), b16t("BPT"), b16t("KPT"), b16t("vTb")
    MM = k.sb("m1_MM", [128, 8, 512], BF16)
    X = [k.sb(f"m1_X{g}", [128, 4, 128], BF16) for g in range(2)]
    NTb = [[k.sb(f"m1_NT{i}_{g}", [128, 4, 128], BF16) for g in range(2)] for i in range(2)]
    Nb = [[k.sb(f"m1_N{i}_{g}", [128, 4, 128], BF16) for g in range(2)] for i in range(2)]
    Vt = k.sb("m1_Vt", [128, 4, 128], BF16)
    BPt = k.sb("m1_BPt", [128, 4, 128], BF16)
    KPt = k.sb("m1_KPt", [128, 4, 128], BF16)
    RhT = k.sb("m1_RhT", [128, 4, 128], F32)
    YhT = k.sb("m1_YhT", [128, 4, 128], F32)
    dtl = f32t("dtl")
    dsq = f32t("dsq")
    rsd = f32t("rsd")
    yaT = [k.sb(f"m1_yaT{i}", [128, 512], BF16) for i in range(2)]
    ssel = [0]
    rot = [0]

    def proj(chunk, pz):
        for ci in range(8):
            k.mm(pz[:, :], win[:, ci, chunk * 128:(chunk + 1) * 128], hT[:, ci, :], start=(ci == 0), stop=(ci == 7))

    def shiftmix(chunk, pz, out_tile):
        zr = zraw[rot[0] % 2]
        zt = ztmp[rot[0] % 2]
        rot[0] += 1
        k.copy("act", zr[:, 1:513], pz[:, :])
        k.copy("pool", zr[:, 0:1], zlast[:, chunk:chunk + 1])
        k.ts("dve", zt[:, :], pz[:, :], omu[:, chunk:chunk + 1], None, ALU.mult)
        k.stt(out_tile[:, :], zr[:, 0:512], mu[:, chunk:chunk + 1], zt[:, :], ALU.mult, ALU.add)
        k.copy("pool", zlast[:, chunk:chunk + 1], zr[:, 512:513])

    def cutpt(n):
        if cut == n:
            raise _Cut()

    try:
      for ti in range(ntiles):
          for s in range(4):
              xs = xs2[s % 2]
              r0 = ti * 512 + s * 128
              k.dma("sp", xs[:, :], xin[r0:r0 + 128, :])
              norm_tile_to_hT(k, c, xs, xn, ssn, (P[0], P[1]), hT, s, gcol)
          cutpt('a')
          proj(12, P[2])
          shiftmix(12, P[2], zs12)
          proj(13, P[3])
          shiftmix(13, P[3], zs13)
          k.act(c12b[0:64, :], zs12[0:64, :], AF.Tanh)
          k.copy("pool", c12b[64:128, :], zs12[64:128, :])
          k.act(sgd[:, :], zs13[:, :], AF.Sigmoid)
          cutpt('b')
          for hp in range(4):
              proj(hp, P[2])
              shiftmix(hp, P[2], r_t)
              proj(4 + hp, P[3])
              shiftmix(4 + hp, P[3], k_t)
              proj(8 + hp, P[2])
              shiftmix(8 + hp, P[2], v_t)
              k.copy("pool", vTb[hp][:, :], v_t[:, :])
              hs = slice(hp * 128, (hp + 1) * 128)
              k.mm(P[4][:, :], w2a2[0:64, hs], c12b[0:64, :])
              k.act(lw[:, :], P[4][:, :], AF.Sigmoid, bias=w0c[:, hp:hp + 1])
              k.ts("dve", lw[:, :], lw[:, :], NEG_E05, None, ALU.mult)
              k.mm(P[5][:, :], w2a2[64:128, hs], c12b[64:128, :])
              k.act(a_t[:, :], P[5][:, :], AF.Sigmoid, bias=a0c[:, hp:hp + 1])
              k.mm(P[6][:, :], g2b[:, hs], sgd[:, :])
              k.copy("act", gT[hp][:, :], P[6][:, :])
              k.ts("dve", kk[:, :], k_t[:, :], kkc[:, hp:hp + 1], None, ALU.mult)
              k.tt("pool", kk2[:, :], kk[:, :], kk[:, :], ALU.mult)
              k.mm(P[4][:, :], bones[:, :], kk2[:, :])
              k.act(kk2[:, :], P[4][:, :], AF.Sqrt)
              k.ts("dve", kk2[:, :], kk2[:, :], 1e-12, None, ALU.max)
              k.recip(kk2[:, :], kk2[:, :])
              k.tt("dve", kkn[:, :], kk[:, :], kk2[:, :], ALU.mult)
              k.ts("dve", ka[:, :], a_t[:, :], kac[:, hp:hp + 1], okac[:, hp:hp + 1], ALU.mult, ALU.add)
              k.tt("pool", kmod[:, :], k_t[:, :], ka[:, :], ALU.mult)
              k.tt("pool", Bq[:, :], kkn[:, :], a_t[:, :], ALU.mult)
              k.stt(rkr[:, :], r_t[:, :], rkc[:, hp:hp + 1], kmod[:, :], ALU.mult, ALU.mult)
              k.mm(P[5][:, :], bones[:, :], rkr[:, :])
              k.tt("dve", bon[hp][:, :], P[5][:, :], v_t[:, :], ALU.mult)
              for cc in range(4):
                  cs = slice(cc * 128, (cc + 1) * 128)
                  k.op("dve", lambda: nc.vector.tensor_tensor_scan(cl.t[:, cs], onesT.t[:, :], lw.t[:, cs], 0.0,
                                                                   ALU.mult, ALU.add),
                       [cl[:, cs]], [onesT[:, :], lw[:, cs]])
              k.act(Pinc[:, :], cl[:, :], AF.Exp)
              k.tt("pool", Pexc[:, :], cl[:, :], lw[:, :], ALU.subtract)
              k.act(Pexc[:, :], Pexc[:, :], AF.Exp)
              k.act(Pinv[:, :], cl[:, :], AF.Exp, scale=-1.0)
              for cc in range(4):
                  cs = slice(cc * 128, (cc + 1) * 128)
                  k.act(E2[:, cs], cl[:, cs], AF.Exp, scale=-1.0, bias=cl[:, cc * 128 + 127:cc * 128 + 128])
              k.copy("pool", PC[:, hp, :], V(Pinc.t[:, :].rearrange("p (c t) -> p c t", t=128)[:, :, 127], Pinc.trk))
              k.stt(ART[hp][:, 0, :], kkn[:, :], -1.0, Pexc[:, :], ALU.mult, ALU.mult)
              k.tt("pool", ART[hp][:, 1, :], r_t[:, :], Pinc[:, :], ALU.mult)
              k.tt("dve", BT[hp][:, :], Bq[:, :], Pinv[:, :], ALU.mult)
              k.tt("pool", KT[hp][:, :], kmod[:, :], Pinv[:, :], ALU.mult)
              k.tt("dve", BPT[hp][:, :], Bq[:, :], E2[:, :], ALU.mult)
              k.tt("pool", KPT[hp][:, :], kmod[:, :], E2[:, :], ALU.mult)
          cutpt('c')
          for cc in range(4):
              cs = slice(cc * 128, (cc + 1) * 128)
              for hp in range(4):
                  k.tr(PB[:, hp * 128:(hp + 1) * 128], ART[hp][:, 0, cs], identb[:, :])
                  k.tr(PB[:, 512 + hp * 128:512 + (hp + 1) * 128], vTb[hp][:, cs], identb[:, :])
              cutpt('d0a')
              for g in range(2):
                  k.copy("dve", X[g][:, :, 0:64],
                         V(PB.t[:, g * 256:(g + 1) * 256].rearrange("p (h d) -> p h d", d=64), PB.trk))
              cutpt('d0b')
              k.copy("act", Vt[:, :, :], V(PB.t[:, 512:1024].rearrange("p (h d) -> p h d", d=128), PB.trk))
              cutpt('d0c')
              for hp in range(4):
                  k.tr(PB[:, hp * 128:(hp + 1) * 128], BPT[hp][:, cs], identb[:, :])
                  k.tr(PB[:, 512 + hp * 128:512 + (hp + 1) * 128], KPT[hp][:, cs], identb[:, :])
              cutpt('d0d')
              k.copy("dve", BPt[:, :, :], V(PB.t[:, 0:512].rearrange("p (h d) -> p h d", d=128), PB.trk))
              cutpt('d0e')
              k.copy("act", KPt[:, :, :], V(PB.t[:, 512:1024].rearrange("p (h d) -> p h d", d=128), PB.trk))
              cutpt('d1')
              for h in range(8):
                  hp, e = h // 2, h % 2
                  es_ = slice(e * 64, (e + 1) * 64)
                  pm = P[h % 2]
                  k.mm(pm[:, 0:128], BT[hp][es_, cs], ART[hp][es_, 0, cs])
                  k.mm(pm[:, 128:256], BT[hp][es_, cs], ART[hp][es_, 1, cs])
                  k.mm(pm[:, 256:384], KT[hp][es_, cs], ART[hp][es_, 0, cs])
                  k.mm(pm[:, 384:512], KT[hp][es_, cs], ART[hp][es_, 1, cs])
                  k.tt("dve", MM[:, h, :], pm[:, :], MSK[:, :], ALU.mult)
                  pn = P[2 + e]
                  k.mm(pn[:, hp * 128:(hp + 1) * 128], ART[hp][es_, 0, cs], BT[hp][es_, cs])
              for e in range(2):
                  for g in range(2):
                      k.tt("dve", V(Nb[0][g].t[:, :, :].rearrange("p (q e) s -> p q e s", e=2)[:, :, e, :], Nb[0][g].trk),
                           V(P[2 + e].t[:, g * 256:(g + 1) * 256].rearrange("p (h s) -> p h s", s=128), P[2 + e].trk),
                           V(MSKL.t[:, 0:256].rearrange("p (h s) -> p h s", s=128), MSKL.trk), ALU.mult)
              cutpt('d2')
              for h in range(8):
                  hp, e = h // 2, h % 2
                  k.mm(P[4][:, h * 64:(h + 1) * 64], MM[:, h, 256:384], Vt[:, hp, e * 64:(e + 1) * 64])
              for g in range(2):
                  k.copy("act", X[g][:, :, 64:128],
                         V(P[4].t[:, g * 256:(g + 1) * 256].rearrange("p (h d) -> p h d", d=64), P[4].trk))
              cutpt('d3')
              def lvl_mm(j, g):
                  ntv = (lambda h: MM[:, h, 0:128]) if j == 0 else (lambda h, t_=NTb[j % 2][g]: t_[:, h % 4, :])
                  nv = (lambda h, t_=Nb[j % 2][g]: t_[:, h % 4, :])
                  hs_ = range(4 * g, 4 * g + 4)
                  for h in hs_:
                      k.mm(P[5 + g][:, (h % 4) * 128:(h % 4 + 1) * 128], ntv(h), X[g][:, h % 4, :])
                  if j < 6:
                      for h in hs_:
                          k.mm(P[2 + g][:, (h % 4) * 128:(h % 4 + 1) * 128], nv(h), ntv(h))
                  if j < 5:
                      for h in hs_:
                          k.mm(P[g][:, (h % 4) * 128:(h % 4 + 1) * 128], ntv(h), nv(h))

              def lvl_evac(j, g):
                  def pv_(b):
                      return V(b.t[:, :].rearrange("p (h s) -> p h s", s=128), b.trk)
                  k.tt("dve", X[g][:, :, :], pv_(P[5 + g]), X[g][:, :, :], ALU.add)
                  if j < 6:
                      k.copy("act", NTb[(j + 1) % 2][g][:, :, :], pv_(P[2 + g]))
                  if j < 5:
                      k.copy("act" if g == 0 else "dve", Nb[(j + 1) % 2][g][:, :, :], pv_(P[g]))

              for j in range(7):
                  lvl_mm(j, 0)
                  lvl_mm(j, 1)
                  lvl_evac(j, 0)
                  lvl_evac(j, 1)
              cutpt('d4')
              for h in range(8):
                  hp, e = h // 2, h % 2
                  es_ = slice(e * 64, (e + 1) * 64)
                  k.mm(P[0][es_, hp * 128:(hp + 1) * 128], X[h // 4][:, h % 4, 0:64], MM[:, h, 128:256])
                  k.mm(P[1][es_, hp * 128:(hp + 1) * 128], X[h // 4][:, h % 4, 64:128], MM[:, h, 128:256], start=True, stop=False,
                       skip_group_check=True)
                  k.mm(P[1][es_, hp * 128:(hp + 1) * 128], Vt[:, hp, es_], MM[:, h, 384:512], start=False, stop=True,
                       skip_group_check=True)
                  k.mm(P[4][es_, hp * 64:(hp + 1) * 64], X[h // 4][:, h % 4, 0:64], BPt[:, hp, es_])
                  k.mm(P[4][es_, 256 + hp * 64:256 + (hp + 1) * 64], BPt[:, hp, es_], X[h // 4][:, h % 4, 64:128], start=True, stop=False,
                       skip_group_check=True)
                  k.mm(P[4][es_, 256 + hp * 64:256 + (hp + 1) * 64], KPt[:, hp, es_], Vt[:, hp, es_], start=False, stop=True,
                       skip_group_check=True)
              for hp in range(4):
                  k.tt("dve", RhT[:, hp, :], P[0][:, hp * 128:(hp + 1) * 128], V(ART[hp].t[:, 1, cs], ART[hp].trk), ALU.add)
              k.copy("act", YhT[:, :, :], V(P[1].t[:, :].rearrange("p (h s) -> p h s", s=128), P[1].trk))
              for hp in range(4):
                  for e in range(2):
                      es_ = slice(e * 64, (e + 1) * 64)
                      k.stt(PhiT2[es_, hp, es_], ident[es_, es_], PC[es_, hp, cc:cc + 1],
                            P[4][es_, hp * 64:(hp + 1) * 64], ALU.mult, ALU.add)
                      k.copy("act", Psi[es_, hp, es_], P[4][es_, 256 + hp * 64:256 + (hp + 1) * 64])
              cutpt('d5')
              Scur = S2[ssel[0] % 2]
              Snew = S2[(ssel[0] + 1) % 2]
              ssel[0] += 1
              for hp in range(4):
                  k.mm(P[5][:, hp * 128:(hp + 1) * 128], Scur[:, hp, :], RhT[:, hp, :])
              for hp in range(4):
                  k.mm(P[6][:, hp * 128:(hp + 1) * 128], PhiT2[:, hp, :], Scur[:, hp, :])
              for hp in range(4):
                  k.tt("dve", YT[hp][:, cs], P[5][:, hp * 128:(hp + 1) * 128], YhT[:, hp, :], ALU.add)
              k.tt("dve", Snew[:, :, :], V(P[6].t[:, :].rearrange("p (h s) -> p h s", s=128), P[6].trk), Psi[:, :, :], ALU.add)
          cutpt('d6')
          for hp in range(4):
              y = YT[hp]
              if dbg is not None:
                  k.dma("sp", dbg[hp, :, ti * 512:(ti + 1) * 512], y[:, :])
              k.mm(P[0][:, :], bones64[:, :], y[:, :])
              k.tt("dve", dtl[:, :], y[:, :], P[0][:, :], ALU.subtract)
              k.act(dsq[:, :], dtl[:, :], AF.Square)
              k.mm(P[1][:, :], bones64[:, :], dsq[:, :])
              k.act(rsd[:, :], P[1][:, :], AF.Sqrt, bias=epsg[:, 0:1])
              k.recip(rsd[:, :], rsd[:, :])
              k.tt("pool", dtl[:, :], dtl[:, :], rsd[:, :], ALU.mult)
              k.stt(dsq[:, :], dtl[:, :], lnw[:, hp:hp + 1], bon[hp][:, :], ALU.mult, ALU.add)
              ya = yaT[hp % 2]
              k.stt(ya[:, :], dsq[:, :], lnb[:, hp:hp + 1], gT[hp][:, :], ALU.add, ALU.mult)
              k.dma("sp", yaT_d[hp, :, ti * 512:(ti + 1) * 512], ya[:, :])
    except _Cut:
        pass
    k.pop_scope()


def mix0_layer(k, c, xin, xout, I):
    yaT_d = k.dram("yaT_d", [4, 128, S], BF16)
    mix0_pass1(k, c, xin, I, yaT_d)
    mix0_pass2(k, c, xin, xout, I, yaT_d)


def rope_tables_np():
    inv = (1.0 / (np.float32(10000.0) ** (np.arange(0, 64, 2, dtype=np.float32) / np.float32(64)))).astype(np.float32)
    ang = (np.arange(S, dtype=np.float32)[:, None] * inv[None, :]).astype(np.float32)
    return np.cos(ang).astype(np.float32), np.sin(ang).astype(np.float32)


EXT_SHAPES = {
    "x": [S, D], "ffn_norm": [2, 2, D], "ffn_w_gate": [2, 2, D, DFF], "ffn_w_up": [2, 2, D, DFF],
    "ffn_w_down": [2, 2, DFF, D], "mix_norm": [2, D],
    "a_w_in": [1, D, 2816], "a_mu": [1, 1792], "a_w0": [1, 512], "a_w2": [1, 64, 512], "a_a0": [1, 512],
    "a_a2": [1, 64, 512], "a_g2": [1, 128, 512], "a_k_k": [1, 512], "a_k_a": [1, 512], "a_r_k": [1, 8, 64],
    "a_ln_w": [1, 512], "a_ln_b": [1, 512], "b_glu_bias": [1, 1024], "b_dw": [1, 31, 512], "b_dw_bias": [1, 512],
    "b_ln_w": [1, 512], "b_ln_b": [1, 512], "e_w_out": [1, D, D],
    "c_w_in": [1, D, 3 * D], "c_q_norm": [1, 64], "c_k_norm": [1, 64], "c_lq1": [1, 64], "c_lk1": [1, 64],
    "c_lq2": [1, 64], "c_lk2": [1, 64], "c_sub_norm": [1, 128], "c_w_out": [1, D, D],
    "rope_cos": [S, 32], "rope_sin": [S, 32],
}
ALL_PHASES = ("f00", "mix0", "f01", "f10", "attn", "f11")


def build_program(phases=ALL_PHASES):
    nc = bass.Bass("TRN2", target_bir_lowering=False)
    es = ExitStack()
    c = Ctx()
    with es:
        k = KB(nc, es)
        c.wtrk = Trk("weights")
        I = {}
        for name, shape in EXT_SHAPES.items():
            I[name] = nc.dram_tensor(name, list(shape), F32, kind="ExternalInput").ap()
        out = Buf(nc.dram_tensor("out", [S, D], F32, kind="ExternalOutput").ap(), "out")
        cur = Buf(I["x"], "x")
        cur.trk = c.wtrk
        scratch = [k.dram("xa", [S, D], F32), k.dram("xb", [S, D], F32)]
        setup_common(k, c)
        for pi, ph in enumerate(phases):
            dst = out if pi == len(phases) - 1 else scratch[pi % 2]
            if ph[0] == "f":
                l, i = int(ph[1]), int(ph[2])
                k.push_scope()
                alloc_ffn(k, c)
                ffn_phase(k, c, cur, dst, I["ffn_w_gate"][l, i], I["ffn_w_up"][l, i], I["ffn_w_down"][l, i],
                          I["ffn_norm"][l, i])
                k.pop_scope()
            elif ph == "attn":
                attn_layer(k, c, cur, dst, I)
            elif ph == "mix0":
                mix0_layer(k, c, cur, dst, I)
            elif ph == "m0p2":
                mix0_pass2(k, c, cur, dst, I, None, use_ya=False)
            elif ph.startswith("m0sim"):
                dbg = Buf(nc.dram_tensor("dbg", [4, 128, S], F32, kind="ExternalOutput").ap(), "dbg")
                yaT_d = k.dram("yaT_d", [4, 128, S], BF16)
                mix0_pass1(k, c, cur, I, yaT_d, ntiles=1, dbg=dbg, cut=ph[5:])
            elif ph == "m0dbg":
                dbg = Buf(nc.dram_tensor("dbg", [4, 128, S], F32, kind="ExternalOutput").ap(), "dbg")
                yaT_d = k.dram("yaT_d", [4, 128, S], BF16)
                mix0_pass1(k, c, cur, I, yaT_d, dbg=dbg)
                mix0_pass2(k, c, cur, dst, I, yaT_d)
            cur = dst
        k.barrier()
        print("instructions:", k.ninst, "sems:", k.nsem)
    return nc


_NC_CACHE = {}


def kernel(_phases=ALL_PHASES, **inputs):
    key = tuple(_phases)
    if key not in _NC_CACHE:
        _NC_CACHE[key] = build_program(_phases)
    nc = _NC_CACHE[key]
    cos, sin = rope_tables_np()
    in_maps = []
    for b in range(8):
        m = {}
        for n in EXT_SHAPES:
            if n == "rope_cos":
                m[n] = cos
            elif n == "rope_sin":
                m[n] = sin
            elif n == "x":
                m[n] = np.ascontiguousarray(np.asarray(inputs[n], dtype=np.float32)[b])
            else:
                m[n] = np.ascontiguousarray(np.asarray(inputs[n], dtype=np.float32))
        in_maps.append(m)
    res = run_bass_kernel_spmd(nc, in_maps, core_ids=list(range(8)))
    _NC_CACHE["last"] = res
    return np.stack([np.asarray(res.results[b]["out"]) for b in range(8)], axis=0)
```
